# Optimizing a Trainium2 kernel written in Bass

```python
import math
import jax, jax.numpy as jnp
from jax import lax
import numpy as np

D_MODEL = 1024
BATCH = 8
SEQ = 2048
DEPTH = 2
DEC_BATCH = 4
DEC_SEQ = 4096
PAST_LEN = 128

MIX_WIDTH = D_MODEL
HEAD_DIM = 64
DIFF_WIDTH = MIX_WIDTH // 2
DIFF_HEADS = DIFF_WIDTH // (2 * HEAD_DIM)
QK_WIDTH = DIFF_HEADS * 2 * HEAD_DIM
V_WIDTH = DIFF_WIDTH
FOURIER_WIDTH = MIX_WIDTH - DIFF_WIDTH
FOURIER_GROUP_DIM = 64
FOURIER_GROUPS = FOURIER_WIDTH // FOURIER_GROUP_DIM
IN_COLS = 2 * QK_WIDTH + V_WIDTH + FOURIER_WIDTH
D_FF = 2816
NUM_BUCKETS = 32
REL_MAX_DISTANCE = 128
Q_BLOCK = 128
N_MOD = 9
EPS = 1e-6
ATTN_SCALE = HEAD_DIM ** -0.5

kernel_name = "hybrid_diffattn_fnet_macaron_encoder"


def rmsnorm(x, g):
    xf = x.astype(jnp.float32)
    y = xf * lax.rsqrt(jnp.mean(xf * xf, axis=-1, keepdims=True) + EPS)
    return (y * g.astype(jnp.float32)).astype(x.dtype)


def modulate(h, shift, scale):
    return h * (1 + scale[:, None, :]) + shift[:, None, :]


def swiglu(h, wi, wo):
    gu = h @ wi
    g, u = jnp.split(gu, 2, axis=-1)
    return (jax.nn.silu(g) * u) @ wo


def rel_bucket(rel):
    nb = NUM_BUCKETS // 2
    max_exact = nb // 2
    ret = (rel > 0).astype(jnp.int32) * nb
    n = jnp.abs(rel)
    nf = jnp.maximum(n, 1).astype(jnp.float32)
    large = max_exact + (jnp.log(nf / max_exact) / math.log(REL_MAX_DISTANCE / max_exact)
                         * (nb - max_exact)).astype(jnp.int32)
    large = jnp.minimum(large, nb - 1)
    return ret + jnp.where(n < max_exact, n, large)


def diff_attention(q, k, v, lam, rel_bias):
    B, H, _, S, dh = q.shape
    n_blk = S // Q_BLOCK
    q_blocks = q.reshape(B, H, 2, n_blk, Q_BLOCK, dh).transpose(3, 0, 1, 2, 4, 5)
    starts = jnp.arange(n_blk, dtype=jnp.int32) * Q_BLOCK
    kpos = jnp.arange(S, dtype=jnp.int32)

    def block(args):
        qb, start = args
        qpos = start + jnp.arange(Q_BLOCK, dtype=jnp.int32)
        buckets = rel_bucket(kpos[None, :] - qpos[:, None])
        bias = rel_bias[buckets].transpose(2, 3, 0, 1)
        s = jnp.einsum('bhjqd,bhjkd->bhjqk', qb, k).astype(jnp.float32) * ATTN_SCALE
        s = s + bias.astype(jnp.float32)
        p = jax.nn.softmax(s, axis=-1)
        a = p[:, :, 0] - lam * p[:, :, 1]
        return jnp.einsum('bhqk,bhkd->bhqd', a.astype(v.dtype), v)

    o = lax.map(block, (q_blocks, starts))
    return o.transpose(1, 0, 3, 2, 4).reshape(B, S, H, 2 * dh)


def mixer(h, w_in, w_out, q_norm, k_norm, lambda_qk, subln, rel_bias, lambda_init):
    B, S, _ = h.shape
    z = h @ w_in
    q, k, v, f = jnp.split(z, [QK_WIDTH, 2 * QK_WIDTH, 2 * QK_WIDTH + V_WIDTH], axis=-1)
    q = rmsnorm(q.reshape(B, S, DIFF_HEADS, 2, HEAD_DIM), q_norm).transpose(0, 2, 3, 1, 4)
    k = rmsnorm(k.reshape(B, S, DIFF_HEADS, 2, HEAD_DIM), k_norm).transpose(0, 2, 3, 1, 4)
    v = v.reshape(B, S, DIFF_HEADS, 2 * HEAD_DIM).transpose(0, 2, 1, 3)
    lqk = lambda_qk.astype(jnp.float32)
    lam = (jnp.exp(jnp.sum(lqk[0] * lqk[1])) - jnp.exp(jnp.sum(lqk[2] * lqk[3]))
           + lambda_init)
    o = diff_attention(q, k, v, lam, rel_bias)
    o = (rmsnorm(o, subln) * (1.0 - lambda_init)).reshape(B, S, DIFF_WIDTH)
    fg = f.reshape(B, S, FOURIER_GROUPS, FOURIER_GROUP_DIM).astype(jnp.float32)
    fo = jnp.fft.fft2(fg, axes=(1, 3), norm='ortho').real.astype(h.dtype).reshape(B, S, FOURIER_WIDTH)
    return jnp.concatenate([o, fo], axis=-1) @ w_out


def trunk(x, c, ada_w, ada_b, norm_ffn1, norm_mix, norm_ffn2, ffn1_wi, ffn1_wo,
          ffn2_wi, ffn2_wo, w_in, w_out, q_norm, k_norm, lambda_qk, subln, rel_bias):
    sc = jax.nn.silu(c)
    for l in range(DEPTH):
        lambda_init = 0.8 - 0.6 * math.exp(-0.3 * l)
        mod = sc @ ada_w[l] + ada_b[l]
        sh1, s1, g1, sh2, s2, g2, sh3, s3, g3 = jnp.split(mod, N_MOD, axis=-1)
        h = modulate(rmsnorm(x, norm_ffn1[l]), sh1, s1)
        x = x + 0.5 * g1[:, None, :] * swiglu(h, ffn1_wi[l], ffn1_wo[l])
        h = modulate(rmsnorm(x, norm_mix[l]), sh2, s2)
        x = x + g2[:, None, :] * mixer(h, w_in[l], w_out[l], q_norm[l], k_norm[l],
                                       lambda_qk[l], subln[l], rel_bias, lambda_init)
        h = modulate(rmsnorm(x, norm_ffn2[l]), sh3, s3)
        x = x + 0.5 * g3[:, None, :] * swiglu(h, ffn2_wi[l], ffn2_wo[l])
    return x


def setup_inputs(seed: int = 0) -> dict:
    key = jax.random.key(seed)
    ks = jax.random.split(key, 24)
    f32 = jnp.float32
    nrm = lambda k, shape, s: jax.random.normal(k, shape, f32) * s
    gain = lambda k, shape: 1.0 + 0.02 * jax.random.normal(k, shape, f32)
    return {
        "x_prompt": nrm(ks[0], (BATCH, SEQ, D_MODEL), 1.0),
        "x_sample": nrm(ks[1], (DEC_BATCH, DEC_SEQ, D_MODEL), 1.0),
        "c_prompt": nrm(ks[2], (BATCH, D_MODEL), 1.0),
        "c_sample": nrm(ks[3], (DEC_BATCH, D_MODEL), 1.0),
        "ada_w": nrm(ks[4], (DEPTH, D_MODEL, N_MOD * D_MODEL), 0.5 * D_MODEL ** -0.5),
        "ada_b": nrm(ks[5], (DEPTH, N_MOD * D_MODEL), 0.02),
        "norm_ffn1": gain(ks[6], (DEPTH, D_MODEL)),
        "norm_mix": gain(ks[7], (DEPTH, D_MODEL)),
        "norm_ffn2": gain(ks[8], (DEPTH, D_MODEL)),
        "ffn1_wi": nrm(ks[9], (DEPTH, D_MODEL, 2 * D_FF), D_MODEL ** -0.5),
        "ffn1_wo": nrm(ks[10], (DEPTH, D_FF, D_MODEL), D_FF ** -0.5),
        "ffn2_wi": nrm(ks[11], (DEPTH, D_MODEL, 2 * D_FF), D_MODEL ** -0.5),
        "ffn2_wo": nrm(ks[12], (DEPTH, D_FF, D_MODEL), D_FF ** -0.5),
        "w_in": nrm(ks[13], (DEPTH, D_MODEL, IN_COLS), D_MODEL ** -0.5),
        "w_out": nrm(ks[14], (DEPTH, MIX_WIDTH, D_MODEL), MIX_WIDTH ** -0.5),
        "q_norm": gain(ks[15], (DEPTH, HEAD_DIM)),
        "k_norm": gain(ks[16], (DEPTH, HEAD_DIM)),
        "lambda_qk": nrm(ks[17], (DEPTH, 4, HEAD_DIM), 0.1),
        "subln": gain(ks[18], (DEPTH, 2 * HEAD_DIM)),
        "rel_bias": nrm(ks[19], (NUM_BUCKETS, DIFF_HEADS, 2), 0.5),
    }


def reference(x_prompt, x_sample, c_prompt, c_sample, ada_w, ada_b, norm_ffn1, norm_mix,
              norm_ffn2, ffn1_wi, ffn1_wo, ffn2_wi, ffn2_wo, w_in, w_out, q_norm, k_norm,
              lambda_qk, subln, rel_bias):
    y_prompt = trunk(x_prompt, c_prompt, ada_w, ada_b, norm_ffn1, norm_mix, norm_ffn2,
                     ffn1_wi, ffn1_wo, ffn2_wi, ffn2_wo, w_in, w_out, q_norm, k_norm,
                     lambda_qk, subln, rel_bias)
    y_sample = trunk(x_sample, c_sample, ada_w, ada_b, norm_ffn1, norm_mix, norm_ffn2,
                     ffn1_wi, ffn1_wo, ffn2_wi, ffn2_wo, w_in, w_out, q_norm, k_norm,
                     lambda_qk, subln, rel_bias)
    return (y_prompt, y_sample)
```

```python
import math
from contextlib import ExitStack

import numpy as np
import ml_dtypes

import concourse.bass as bass
import concourse.mybir as mybir
from concourse.bass_utils import run_bass_kernel_spmd

F32 = mybir.dt.float32
BF16 = mybir.dt.bfloat16
AF = mybir.ActivationFunctionType
ALU = mybir.AluOpType

NCORES = 8
NT = 4096
D = 1024
KC = 8
DFF = 2816
FC = 22
TT = 512
NTT = NT // TT
NKC = NT // 128
EPS = 1e-6
NEG = -30000.0
HD = 64
NH = 4


class Sem:
    __slots__ = ("h", "val", "name")

    def __init__(self, h, name):
        self.h = h
        self.val = 0
        self.name = name


class Buf:
    __slots__ = ("name", "w", "rs")

    def __init__(self, name=""):
        self.name = name
        self.w = None
        self.rs = {}


class Eng:
    def __init__(self, name, sem):
        self.name = name
        self.sem = sem
        self.ops = []
        self.waited = {}

    def issue(self, fn, reads=(), writes=(), inc=True, dsem=None, extra=()):
        need = {}

        def add(ev):
            if ev is None:
                return
            s, v = ev
            if v > need.get(s, 0):
                need[s] = v
        for b in reads:
            add(b.w)
        for b in writes:
            add(b.w)
            for s, v in b.rs.items():
                add((s, v))
        for ev in extra:
            add(ev)
        waits = []
        for s, v in need.items():
            if self.name == "pe" and s is self.sem:
                continue
            if self.waited.get(s, 0) < v:
                self.waited[s] = v
                waits.append((s, v))
        if dsem is not None:
            dsem.val += 16
            ev = (dsem, dsem.val)
            do_inc = (dsem, 16)
        elif inc:
            self.sem.val += 1
            ev = (self.sem, self.sem.val)
            do_inc = (self.sem, 1)
        else:
            ev = (self.sem, self.sem.val + 1)
            do_inc = None
        self.ops.append((waits, fn, do_inc))
        s, v = ev
        for b in reads:
            if b.rs.get(s, 0) < v:
                b.rs[s] = v
        for b in writes:
            b.w = ev
            b.rs = {}
        return ev

    def wait_event(self, ev):
        if ev is None:
            return
        s, v = ev
        if self.name == "pe" and s is self.sem:
            return
        if self.waited.get(s, 0) < v:
            self.waited[s] = v
            self.ops.append(([(s, v)], None, None))

    def replay(self, e):
        for waits, fn, do_inc in self.ops:
            for s, v in waits:
                e.wait_ge(s.h, v)
            if fn is None:
                continue
            ins = fn(e)
            if do_inc is not None:
                ins.then_inc(do_inc[0].h, do_inc[1])


def _rel_bucket_np(rel):
    try:
        import jax
        import jax.numpy as jnp
        cpu = jax.devices("cpu")[0]
        with jax.default_device(cpu):
            rel_j = jnp.asarray(rel, dtype=jnp.int32)
            nb = 16
            max_exact = 8
            ret = (rel_j > 0).astype(jnp.int32) * nb
            n = jnp.abs(rel_j)
            nf = jnp.maximum(n, 1).astype(jnp.float32)
            large = max_exact + (jnp.log(nf / max_exact) / math.log(128 / max_exact)
                                 * (nb - max_exact)).astype(jnp.int32)
            large = jnp.minimum(large, nb - 1)
            out = ret + jnp.where(n < max_exact, n, large)
            return np.asarray(out)
    except Exception:
        rel = np.asarray(rel, dtype=np.int32)
        nb, max_exact = 16, 8
        ret = (rel > 0).astype(np.int32) * nb
        n = np.abs(rel)
        nf = np.maximum(n, 1).astype(np.float32)
        large = max_exact + (np.log(nf / np.float32(max_exact)) / np.float32(math.log(128 / max_exact))
                             * np.float32(nb - max_exact)).astype(np.int32)
        large = np.minimum(large, nb - 1)
        return ret + np.where(n < max_exact, n, large)


_CONST_CACHE = {}


def _host_consts():
    if "c" in _CONST_CACHE:
        return _CONST_CACHE["c"]
    c = {}
    c["ident"] = np.eye(128, dtype=np.float32)
    c["identb"] = np.eye(128, dtype=np.float32).astype(ml_dtypes.bfloat16)
    c["jmat"] = np.ascontiguousarray(np.eye(128, dtype=np.float32)[::-1])
    y = np.arange(1280)
    bk = _rel_bucket_np(639 - y)
    oh = np.zeros((32, 1280), np.float32)
    oh[bk, y] = 1.0
    c["onehot"] = oh
    s15 = np.zeros((32, 128), np.float32); s15[15] = 1.0
    s31 = np.zeros((32, 128), np.float32); s31[31] = 1.0
    c["sel15"] = s15
    c["sel31"] = s31
    selL = np.zeros((NKC, NTT), np.float32)
    selR = np.zeros((NKC, NTT), np.float32)
    for kc in range(NKC):
        for qt in range(NTT):
            d = kc - 4 * qt
            if d <= -2:
                selL[kc, qt] = 1.0
            elif d >= 5:
                selR[kc, qt] = 1.0
    c["selL"] = np.ascontiguousarray(np.broadcast_to(selL.reshape(1, -1), (128, NKC * NTT)))
    c["selR"] = np.ascontiguousarray(np.broadcast_to(selR.reshape(1, -1), (128, NKC * NTT)))
    m_s = np.zeros((NKC, NTT), np.float32)
    m_p = np.zeros((NKC, NTT), np.float32)
    for kc in range(NKC):
        for qt in range(NTT):
            if (kc // 16) != (qt // 4):
                m_p[kc, qt] = NEG
    c["mask_s"] = np.ascontiguousarray(np.broadcast_to(m_s.reshape(1, -1), (128, NKC * NTT)))
    c["mask_p"] = np.ascontiguousarray(np.broadcast_to(m_p.reshape(1, -1), (128, NKC * NTT)))
    bd = np.zeros((128, 128), np.float32)
    bd[:64, :64] = 1.0
    bd[64:, 64:] = 1.0
    c["onesbd"] = bd.astype(ml_dtypes.bfloat16)
    a = np.arange(64)
    ang = 2.0 * np.pi * np.outer(a, a) / 64.0
    c64 = np.cos(ang) / 8.0
    s64 = np.sin(ang) / 8.0
    bdc = np.zeros((128, 128)); bds = np.zeros((128, 128))
    bdc[:64, :64] = c64; bdc[64:, 64:] = c64
    bds[:64, :64] = -s64; bds[64:, 64:] = -s64
    c["bdc"] = bdc.astype(np.float32).astype(ml_dtypes.bfloat16)
    c["bds"] = bds.astype(np.float32).astype(ml_dtypes.bfloat16)

    def dft_pack(S):
        nseq = NT // S
        s = np.arange(S, dtype=np.int64)
        m = (np.outer(s, s) % S).astype(np.float64)
        cs = (np.cos(2 * np.pi * m / S) / math.sqrt(S)).astype(np.float32)
        sn = (np.sin(2 * np.pi * m / S) / math.sqrt(S)).astype(np.float32)
        C = np.zeros((NT, NT), np.float32)
        Sn = np.zeros((NT, NT), np.float32)
        for q in range(nseq):
            C[q * S:(q + 1) * S, q * S:(q + 1) * S] = cs
            Sn[q * S:(q + 1) * S, q * S:(q + 1) * S] = sn
        out = np.zeros((8, 8, 128, 4, 2, 512), ml_dtypes.bfloat16)
        for mi, M in enumerate((C, Sn)):
            M6 = M.reshape(8, 4, 128, 8, 512)
            out[:, :, :, :, mi, :] = M6.transpose(3, 0, 2, 1, 4).astype(ml_dtypes.bfloat16)
        return np.ascontiguousarray(out.reshape(64, 128, 4 * 2 * 512))
    c["dft_s"] = dft_pack(4096)
    c["dft_p"] = dft_pack(2048)
    _CONST_CACHE["c"] = c
    return c


class Builder:
    def __init__(self, phases):
        self.phases = phases
        self.nc = bass.Bass("TRN2", target_bir_lowering=False)
        self.gst = ExitStack()
        self.nsem = 0
        self.dma_sems = []
        self.bank_rr = 0

    def sem(self, name):
        self.nsem += 1
        return Sem(self.gst.enter_context(self.nc.semaphore(f"s{self.nsem}_{name}")), name)

    def dsem(self, name, barrier=True):
        if not hasattr(self, "semcache"):
            self.semcache = {}
        if name in self.semcache:
            return self.semcache[name]
        s = self.sem(name)
        if barrier:
            self.dma_sems.append(s)
        else:
            self.nobar_sems = getattr(self, "nobar_sems", []) + [s]
        self.semcache[name] = s
        return s

    def din(self, name, shape, dt):
        return self.nc.dram_tensor(name, list(shape), dt, kind="ExternalInput").ap()

    def dscr(self, name, shape, dt):
        return self.nc.dram_tensor(name, list(shape), dt).ap()

    def sb(self, st, name, shape, dt):
        self.nsb = getattr(self, "nsb", 0) + 1
        return st.enter_context(self.nc.sbuf_tensor(f"{name}_{self.nsb}", list(shape), dt))

    def nextbank(self):
        b = self.bank_rr
        self.bank_rr = (b + 1) % 8
        return b

    def mm(self, out, lhsT, rhs, start, stop, reads, writes=(), inc=False, **kw):
        return self.pe.issue(lambda e: e.matmul(out, lhsT=lhsT, rhs=rhs, start=start, stop=stop, **kw),
                             reads=reads, writes=writes, inc=inc)

    def tr(self, out, in_, ident, reads, writes=(), inc=True):
        return self.pe.issue(lambda e: e.transpose(out=out, in_=in_, identity=ident),
                             reads=reads, writes=writes, inc=inc)

    def actf(self, out, in_, func, reads, writes, bias=None, scale=None, accum_out=None):
        kw = {}
        if bias is not None:
            kw["bias"] = bias
        if scale is not None:
            kw["scale"] = scale
        if accum_out is not None:
            kw["accum_out"] = accum_out
        return self.act.issue(lambda e: e.activation(out=out, in_=in_, func=func, **kw),
                              reads=reads, writes=writes)

    def ts(self, eng, out, in0, s1, s2, op0, op1, reads, writes):
        if op1 is None:
            return eng.issue(lambda e: e.tensor_scalar(out=out, in0=in0, scalar1=s1, scalar2=None, op0=op0),
                             reads=reads, writes=writes)
        return eng.issue(lambda e: e.tensor_scalar(out=out, in0=in0, scalar1=s1, scalar2=s2, op0=op0, op1=op1),
                         reads=reads, writes=writes)

    def stt(self, eng, out, in0, scalar, in1, op0, op1, reads, writes):
        return eng.issue(lambda e: e.scalar_tensor_tensor(out=out, in0=in0, scalar=scalar, in1=in1, op0=op0, op1=op1),
                         reads=reads, writes=writes)

    def tt(self, eng, out, in0, in1, op, reads, writes):
        return eng.issue(lambda e: e.tensor_tensor(out=out, in0=in0, in1=in1, op=op), reads=reads, writes=writes)

    def cp(self, eng, out, in_, reads, writes):
        if eng is self.act:
            return eng.issue(lambda e: e.activation(out=out, in_=in_, func=AF.Identity), reads=reads, writes=writes)
        return eng.issue(lambda e: e.tensor_copy(out=out, in_=in_), reads=reads, writes=writes)

    def dma(self, eng, out, in_, reads, writes, dsem):
        return eng.issue(lambda e: e.dma_start(out=out, in_=in_), reads=reads, writes=writes, dsem=dsem)

    def barrier(self):
        evs = []
        for e in self.engs:
            if e.sem.val > 0:
                evs.append((e.sem, e.sem.val))
        for s in self.dma_sems:
            if s.val > 0:
                evs.append((s, s.val))
        for e in self.engs:
            for ev in evs:
                e.wait_event(ev)

    def build(self):
        nc = self.nc
        g = self.gst
        with g:
            self._build()
        return nc

    def _build(self):
        nc = self.nc
        d = {}
        d["x"] = self.din("x", [NT, D], F32)
        d["c2"] = self.din("c2", [16, 128], F32)
        d["ada_w"] = self.din("ada_w", [2, D, 9 * D], F32)
        d["ada_b"] = self.din("ada_b", [2, 72, 128], F32)
        for n in ("norm_ffn1", "norm_mix", "norm_ffn2"):
            d[n] = self.din(n, [2, 8, 128], F32)
        d["ffn1_wi"] = self.din("ffn1_wi", [2, D, 2 * DFF], F32)
        d["ffn1_wo"] = self.din("ffn1_wo", [2, DFF, D], F32)
        d["ffn2_wi"] = self.din("ffn2_wi", [2, D, 2 * DFF], F32)
        d["ffn2_wo"] = self.din("ffn2_wo", [2, DFF, D], F32)
        d["w_in"] = self.din("w_in", [2, D, 2048], F32)
        d["w_out"] = self.din("w_out", [2, D, D], F32)
        d["q_norm"] = self.din("q_norm", [2, 1, 64], F32)
        d["k_norm"] = self.din("k_norm", [2, 1, 64], F32)
        d["lambda_qk"] = self.din("lambda_qk", [2, 1, 256], F32)
        d["subln"] = self.din("subln", [2, 1, 128], F32)
        d["rel_bias"] = self.din("rel_bias", [32, 8], F32)
        d["ident"] = self.din("ident", [128, 128], F32)
        d["identb"] = self.din("identb", [128, 128], BF16)
        d["jmat"] = self.din("jmat", [128, 128], F32)
        d["onehot"] = self.din("onehot", [32, 1280], F32)
        d["sel15"] = self.din("sel15", [32, 128], F32)
        d["sel31"] = self.din("sel31", [32, 128], F32)
        d["selL"] = self.din("selL", [128, 256], F32)
        d["selR"] = self.din("selR", [128, 256], F32)
        d["mask"] = self.din("mask", [128, 256], F32)
        d["onesbd"] = self.din("onesbd", [128, 128], BF16)
        d["bdc"] = self.din("bdc", [128, 128], BF16)
        d["bds"] = self.din("bds", [128, 128], BF16)
        d["dftm"] = self.din("dftm", [64, 128, 4096], BF16)
        self.y = nc.dram_tensor("y", [NT, D], F32, kind="ExternalOutput").ap()
        self.d = d
        sc = {}
        for l in range(2):
            sc[(l, "wi", 0)] = self.dscr(f"wi1b{l}", [D, 2 * DFF], BF16)
            sc[(l, "wo", 0)] = self.dscr(f"wo1b{l}", [DFF, D], BF16)
            sc[(l, "wi", 2)] = self.dscr(f"wi2b{l}", [D, 2 * DFF], BF16)
            sc[(l, "wo", 2)] = self.dscr(f"wo2b{l}", [DFF, D], BF16)
            sc[(l, "win")] = self.dscr(f"winb{l}", [D, 2048], BF16)
            sc[(l, "wout")] = self.dscr(f"woutb{l}", [D, D], BF16)
        sc["xT"] = self.dscr("xTs", [KC, 128, NT], F32)
        sc["QT"] = self.dscr("QTs", [NH, 128, NT], BF16)
        sc["KT"] = self.dscr("KTs", [NH, 128, NT], BF16)
        sc["V"] = self.dscr("Vs", [NT, NH * 129], BF16)
        sc["XC"] = self.dscr("XCs", [NT, 512], BF16)
        sc["XS"] = self.dscr("XSs", [NT, 512], BF16)
        sc["FOT"] = self.dscr("FOTs", [4, 128, NT], BF16)
        sc["G"] = self.dscr("Gs", [8, 1280], F32)
        self.sc = sc
        self.scb = {k: Buf(str(k)) for k in sc}
        self.xTb = [Buf(f"xTs{t}") for t in range(NTT)]

        self.pe = Eng("pe", self.sem("pe"))
        self.act = Eng("act", self.sem("act"))
        self.dve = Eng("dve", self.sem("dve"))
        self.pool = Eng("pool", self.sem("pool"))
        self.sp = Eng("sp", self.sem("sp"))
        self.engs = [self.pe, self.act, self.dve, self.pool, self.sp]

        self.ps = [self.gst.enter_context(nc.psum_tensor(f"ps{b}", [128, 512], F32)) for b in range(8)]
        self.psb = [Buf(f"ps{b}") for b in range(8)]

        P = self.gst
        self.ident = self.sb(P, "ident", [128, 128], F32)
        self.identb = self.sb(P, "identb", [128, 128], BF16)
        self.onesb = self.sb(P, "onesb", [128, 128], BF16)
        self.onesbd = self.sb(P, "onesbd", [128, 128], BF16)
        self.bdc = self.sb(P, "bdc", [128, 128], BF16)
        self.bds = self.sb(P, "bds", [128, 128], BF16)
        self.strips = self.sb(P, "strips", [128, 8, 1152], F32)
        self.table = self.sb(P, "table", [128, 8, 256], F32)
        self.modc = [self.sb(P, f"modc{l}", [128, 2 * 3 * 3 * 8], F32) for l in range(2)]
        self.qkcol = [self.sb(P, f"qkcol{l}", [128, 4], F32) for l in range(2)]
        self.cbuf = Buf("consts")
        self.epsc = self.sb(P, "epsc", [128, 1], F32)
        self.colsL = [self.sb(P, f"colsL{l}", [128, 99], F32) for l in range(2)]
        self.colb = Buf("cols")
        self.scT = self.sb(P, "scT", [128, 16], BF16)
        self.scTb = Buf("scT")
        self.modT = [self.sb(P, f"modT{l}", [128, 144], F32) for l in range(2)]
        self.modb = Buf("modT")
        self.mcbl = [Buf("modc0"), Buf("modc1")]

        self.prologue()
        first = True
        np_ = len(self.phases)
        for i, (l, ph) in enumerate(self.phases):
            last = (i == np_ - 1)
            self.barrier()
            if ph == "f1":
                self.ffn_phase(l, 0, first, last)
            elif ph == "f2":
                self.ffn_phase(l, 2, first, last)
            elif ph == "mix":
                self.mix_a(l, first)
                self.barrier()
                self.mix_d(l)
                self.barrier()
                self.mix_c(l, last)
            first = False
        self.barrier()
        with nc.Block() as block:
            @block.sync
            def _(e):
                self.sp.replay(e)

            @block.tensor
            def _(e):
                self.pe.replay(e)

            @block.scalar
            def _(e):
                self.act.replay(e)

            @block.vector
            def _(e):
                self.dve.replay(e)

            @block.gpsimd
            def _(e):
                self.pool.replay(e)

    def casts_now(self, keys):
        todo = []
        for k in keys:
            for item in self.cast_rest:
                if item[0] == k:
                    todo.append(item)
        for item in todo:
            self.cast_rest.remove(item)
        if todo:
            self.issue_casts(todo)

    def mcol(self, l, seg, sub, kind, kc):
        idx = ((seg * 3 + sub) * 3 + kind) * 8 + kc
        return self.modc[l][:, idx:idx + 1]

    def prologue(self):
        nc, d, sc = self.nc, self.d, self.sc
        pe, act, dve, pool, sp = self.pe, self.act, self.dve, self.pool, self.sp
        self.wsem = {}
        cast_list = []
        for l in range(2):
            cast_list += [((l, "wi", 0), d["ffn1_wi"][l]), ((l, "wo", 0), d["ffn1_wo"][l]),
                          ((l, "win"), d["w_in"][l]), ((l, "wout"), d["w_out"][l]),
                          ((l, "wi", 2), d["ffn2_wi"][l]), ((l, "wo", 2), d["ffn2_wo"][l])]

        def issue_casts(items):
            for key, src in items:
                s = self.dsem("w" + "_".join(map(str, key)), barrier=False)
                dst = sc[key]
                if src.shape[-1] > 2048:
                    src = src.rearrange("r (a b) -> (r a) b", b=1408)
                    dst = dst.rearrange("r (a b) -> (r a) b", b=1408)
                self.dma(pool, dst, src, [], [self.scb[key]], s)
        with ExitStack() as st:
            sb = lambda n, s, t: self.sb(st, n, s, t)
            cs = self.dsem("pconst")
            cb = self.cbuf
            small_evs = []
            for dst, src in ((self.ident, d["ident"]), (self.identb, d["identb"]), (self.onesbd, d["onesbd"]),
                             (self.bdc, d["bdc"]), (self.bds, d["bds"])):
                small_evs.append(self.dma(sp, dst[:, :], src[:, :], [], [], cs))
            jm = sb("jm", [128, 128], F32)
            rb = sb("rb", [32, 8], F32)
            oh = sb("oh", [32, 1280], F32)
            s15 = sb("s15", [32, 128], F32)
            s31 = sb("s31", [32, 128], F32)
            selL = sb("selL", [128, 256], F32)
            selR = sb("selR", [128, 256], F32)
            mask = sb("mask", [128, 256], F32)
            for dst, src in ((jm, d["jmat"]), (rb, d["rel_bias"]), (oh, d["onehot"]), (s15, d["sel15"]),
                             (s31, d["sel31"]), (selL, d["selL"]), (selR, d["selR"]), (mask, d["mask"])):
                small_evs.append(self.dma(sp, dst[:, :], src[:, :], [], [], cs))
            pool.issue(lambda e: e.memset(self.onesb[:, :], 1.0), writes=[cb])
            pool.issue(lambda e: e.memset(self.epsc[:, :], EPS), writes=[cb])
            onesrow = sb("onesrow", [1, 128], F32)
            pool.issue(lambda e: e.memset(onesrow[:, :], 1.0), writes=[cb])
            stA = sb("stA", [16, 128], F32)
            stL = [sb(f"stL{l}", [99, 128], F32) for l in range(2)]
            stb = Buf("st")
            small_evs.append(self.dma(sp, stA[:, :], d["c2"][:, :], [], [], cs))
            for l in range(2):
                small_evs.append(self.dma(sp, stL[l][0:72, :], d["ada_b"][l], [], [], cs))
                small_evs.append(self.dma(sp, stL[l][72:80, :], d["norm_ffn1"][l], [], [], cs))
                small_evs.append(self.dma(sp, stL[l][80:88, :], d["norm_mix"][l], [], [], cs))
                small_evs.append(self.dma(sp, stL[l][88:96, :], d["norm_ffn2"][l], [], [], cs))
                small_evs.append(self.dma(sp, stL[l][96:97, 0:64], d["q_norm"][l], [], [], cs))
                small_evs.append(self.dma(sp, stL[l][96:97, 64:128], d["q_norm"][l], [], [], cs))
                small_evs.append(self.dma(sp, stL[l][97:98, 0:64], d["k_norm"][l], [], [], cs))
                small_evs.append(self.dma(sp, stL[l][97:98, 64:128], d["k_norm"][l], [], [], cs))
                small_evs.append(self.dma(sp, stL[l][98:99, :], d["subln"][l], [], [], cs))
            lq = [sb(f"lq{l}", [1, 256], F32) for l in range(2)]
            for l in range(2):
                small_evs.append(self.dma(sp, lq[l][:, :], d["lambda_qk"][l], [], [], cs))
            all_small = small_evs[-1]
            for e_ in (pe, act, dve, pool):
                e_.wait_event(all_small)
            colsA = sb("colsA", [128, 16], F32)
            colsL = self.colsL
            colb = self.colb
            b0 = self.nextbank()
            self.tr(self.ps[b0][:, 0:16], stA[:, :], self.ident[0:16, 0:16], [stb, cb], [self.psb[b0]])
            self.cp(dve, colsA[:, :], self.ps[b0][:, 0:16], [self.psb[b0]], [colb])
            for l in range(2):
                b0 = self.nextbank()
                self.tr(self.ps[b0][:, 0:99], stL[l][:, :], self.ident[0:99, 0:99], [stb, cb], [self.psb[b0]])
                self.cp(dve, colsL[l][:, :], self.ps[b0][:, 0:99], [self.psb[b0]], [colb])
            self.actf(self.scT[:, :], colsA[:, :], AF.Silu, [colb], [self.scTb])
            bk = self.nextbank()
            g0 = self.ada_gen(0, st, 1024, bk, nslots=2, cast_engs=[dve, act])
            next(g0)
            pool.wait_event(self.ada_evs[1])
            issue_casts(cast_list[0:2])
            for _ in g0:
                pass
            self.issue_casts = issue_casts
            self.cast_rest = cast_list[4:]
            self.cast_done = set()
            qkb = Buf("qkcol")
            self.qkb = qkb
            lam_t = sb("lam_t", [1, 64], F32)
            lam_s = sb("lam_s", [1, 4], F32)
            lamb = Buf("lam")
            for l in range(2):
                li = 0.8 - 0.6 * math.exp(-0.3 * l)
                self.cp(dve, self.qkcol[l][:, 0:2], colsL[l][:, 96:98], [colb], [qkb])
                self.ts(dve, self.qkcol[l][:, 2:3], colsL[l][:, 98:99], 1.0 - li, None, ALU.mult, None, [colb], [qkb])
                for j in range(2):
                    self.tt(dve, lam_t[:, :], lq[l][:, 128 * j:128 * j + 64], lq[l][:, 128 * j + 64:128 * j + 128],
                            ALU.mult, [stb], [lamb])
                    dve.issue(lambda e, j=j: e.reduce_sum(out=lam_s[:, j:j + 1], in_=lam_t[:, :],
                                                          axis=mybir.AxisListType.X), reads=[lamb], writes=[lamb])
                self.actf(lam_s[:, 2:4], lam_s[:, 0:2], AF.Exp, [lamb], [lamb])
                self.tt(dve, lam_s[:, 0:1], lam_s[:, 3:4], lam_s[:, 2:3], ALU.subtract, [lamb], [lamb])
                self.ts(dve, lam_s[:, 1:2], lam_s[:, 0:1], -li, None, ALU.add, None, [lamb], [lamb])
                bk = self.nextbank()
                self.mm(self.ps[bk][:, 0:1], onesrow[:, :], lam_s[:, 1:2], True, True, [cb, lamb], [self.psb[bk]], inc=True)
                self.cp(dve, self.qkcol[l][:, 3:4], self.ps[bk][:, 0:1], [self.psb[bk]], [qkb])
            gsb = sb("gsb", [8, 1280], F32)
            gb = Buf("gsb")
            for (c0, c1) in ((0, 512), (512, 1024), (1024, 1280)):
                bk = self.nextbank()
                self.mm(self.ps[bk][0:8, 0:c1 - c0], rb[:, :], oh[:, c0:c1], True, True, [cb], [self.psb[bk]], inc=True)
                self.cp(dve, gsb[:, c0:c1], self.ps[bk][0:8, 0:c1 - c0], [self.psb[bk]], [gb])
            gs = self.dsem("gs")
            self.dma(sp, sc["G"][:, :], gsb[:, :], [gb], [self.scb["G"]], gs)
            hank = [sb(f"hank{i}", [128, 1152], F32) for i in range(2)]
            hb = [Buf(f"hank{i}") for i in range(2)]
            hs = [self.dsem(f"hank{i}") for i in range(2)]
            sb_ = Buf("strips")
            self.stripb = sb_
            for hj in range(8):
                slot = hj % 2
                src = bass.AP(sc["G"].tensor, hj * 1280, [[1, 128], [1, 1152]])
                last_hank = self.dma(sp, hank[slot][:, :], src, [self.scb["G"]], [hb[slot]], hs[slot])
                for (c0, c1) in ((0, 512), (512, 1024), (1024, 1152)):
                    bk = self.nextbank()
                    self.mm(self.ps[bk][:, 0:c1 - c0], jm[:, :], hank[slot][:, c0:c1], True, True,
                            [cb, hb[slot]], [self.psb[bk]], inc=True)
                    self.cp(act if (c0 == 512) else dve, self.strips[:, hj, c0:c1], self.ps[bk][:, 0:c1 - c0],
                            [self.psb[bk]], [sb_])
            pool.wait_event(last_hank)
            issue_casts(cast_list[2:4])
            for hj in range(8):
                self.actf(self.strips[:, hj, :], self.strips[:, hj, :], AF.Exp, [sb_], [sb_])
            bcols = sb("bcols", [128, 16], F32)
            bcb = Buf("bcols")
            bk = self.nextbank()
            self.mm(self.ps[bk][:, 0:8], s15[:, :], rb[:, :], True, True, [cb], [self.psb[bk]], inc=False)
            self.mm(self.ps[bk][:, 8:16], s31[:, :], rb[:, :], False, True, [cb], [self.psb[bk]], inc=True,
                    skip_group_check=True)
            self.cp(dve, bcols[:, :], self.ps[bk][:, 0:16], [self.psb[bk]], [bcb])
            tmpt = sb("tmpt", [128, 256], F32)
            tb = Buf("tmpt")
            self.tableb = Buf("table")
            for hj in range(8):
                self.stt(dve, tmpt[:, :], selL[:, :], bcols[:, hj:hj + 1], mask[:, :], ALU.mult, ALU.add,
                         [cb, bcb], [tb])
                self.stt(dve, self.table[:, hj, :], selR[:, :], bcols[:, 8 + hj:9 + hj], tmpt[:, :], ALU.mult, ALU.add,
                         [cb, bcb, tb], [self.tableb])
            self.barrier()

    def ada_gen(self, l, st, gcols, bank, nslots=2, cast_engs=None):
        d = self.d
        pe, dve, sp, act = self.pe, self.dve, self.sp, self.act
        if cast_engs is None:
            cast_engs = [dve]
        ng = (9 * D) // gcols
        mpg = gcols // 128
        adaw = [self.sb(st, f"adaw{i}", [128, 8, gcols], F32) for i in range(nslots)]
        adab = [Buf(f"adaw{i}") for i in range(nslots)]
        adas = [self.dsem(f"adaw{i}") for i in range(nslots)]
        adah = [self.sb(st, f"adah{i}", [128, 8, gcols], BF16) for i in range(nslots)]
        adahb = [Buf(f"adah{i}") for i in range(nslots)]
        colsL, modT, modb, colb = self.colsL, self.modT, self.modb, self.colb
        scv = self.scT[:, :].rearrange("p (s k) -> p k s", k=8)
        self.ada_evs = []

        def load(gi):
            slot = gi % nslots
            self.last_ada_ev = self.dma(
                sp, adaw[slot][:, :, :],
                d["ada_w"][l][:, gi * gcols:(gi + 1) * gcols].rearrange("(kc p) n -> p kc n", p=128),
                [], [adab[slot]], adas[slot])
            self.ada_evs.append(self.last_ada_ev)
        for gi in range(min(nslots, ng)):
            load(gi)
        for gi in range(ng):
            slot = gi % nslots
            ne = len(cast_engs)
            for ci, eng in enumerate(cast_engs):
                k0, k1 = ci * 8 // ne, (ci + 1) * 8 // ne
                self.cp(eng, adah[slot][:, k0:k1, :], adaw[slot][:, k0:k1, :], [adab[slot]], [adahb[slot]])
            if gi + nslots < ng:
                load(gi + nslots)
            for mcl in range(mpg):
                for kc in range(KC):
                    first = (mcl == 0 and kc == 0)
                    lastg = (mcl == mpg - 1 and kc == KC - 1)
                    self.mm(self.ps[bank][:, 2 * mcl:2 * mcl + 2], adah[slot][:, kc, mcl * 128:(mcl + 1) * 128],
                            scv[:, kc, :], first, kc == KC - 1, [adahb[slot], self.scTb],
                            [self.psb[bank]] if (first or lastg) else [], inc=lastg, skip_group_check=True)
            pv = self.ps[bank][:, 0:2 * mpg].rearrange("p (m s) -> p m s", s=2)
            m0 = gi * mpg
            mv = modT[l][:, 2 * m0:2 * (m0 + mpg)].rearrange("p (m s) -> p m s", s=2)
            for seg in range(2):
                self.tt(dve, mv[:, :, seg], pv[:, :, seg], colsL[l][:, m0:m0 + mpg], ALU.add,
                        [self.psb[bank], colb], [modb])
            yield
        mcb = self.mcbl[l]
        mv = modT[l][:, :].rearrange("p (m s) -> p m s", s=2)
        for seg in range(2):
            for sub in range(3):
                base = ((seg * 3 + sub) * 3) * 8
                A = self.modc[l][:, base:base + 8]
                Bc = self.modc[l][:, base + 8:base + 16]
                G = self.modc[l][:, base + 16:base + 24]
                shift = mv[:, (3 * sub) * 8:(3 * sub) * 8 + 8, seg]
                scale = mv[:, (3 * sub + 1) * 8:(3 * sub + 1) * 8 + 8, seg]
                gate = mv[:, (3 * sub + 2) * 8:(3 * sub + 2) * 8 + 8, seg]
                gain = colsL[l][:, 72 + 8 * sub:80 + 8 * sub]
                self.stt(dve, A, scale, 1.0, gain, ALU.add, ALU.mult, [modb, colb], [mcb])
                self.cp(dve, Bc, shift, [modb], [mcb])
                self.ts(dve, G, gate, 1.0 if sub == 1 else 0.5, None, ALU.mult, None, [modb], [mcb])
        yield

    def load_xT(self, tt, xT, xb, xsem, first, xin=None, xinb=None, xins=None, store_scratch=False, st_sem=None):
        sp, pe, dve, act = self.sp, self.pe, self.dve, self.act
        if not first:
            src = self.sc["xT"][:, :, tt * TT:(tt + 1) * TT].rearrange("kc p t -> p kc t")
            self.dma(sp, xT[:, :, :], src, [self.xTb[tt]], [xb], xsem)
            return
        src = self.d["x"][tt * TT:(tt + 1) * TT, :].rearrange("(s p) c -> p s c", p=128)
        self.dma(sp, xin[:, :, :], src, [], [xinb], xins)
        for kc in range(KC):
            bk = self.nextbank()
            for s in range(4):
                self.tr(self.ps[bk][:, s * 128:(s + 1) * 128], xin[:, s, kc * 128:(kc + 1) * 128], self.ident[:, :],
                        [xinb], [self.psb[bk]] if s in (0, 3) else [], inc=(s == 3))
            self.cp(act if kc % 2 else dve, xT[:, kc, :], self.ps[bk][:, :], [self.psb[bk]], [xb])
        if store_scratch:
            dst = self.sc["xT"][:, :, tt * TT:(tt + 1) * TT].rearrange("kc p t -> p kc t")
            self.dma(self.pool, dst, xT[:, :, :], [xb], [self.xTb[tt]], st_sem)

    def store_xT(self, tt, xT, xb, xsem, last, yout=None, youtb=None, youts=None):
        pool, pe, dve, act = self.pool, self.pe, self.dve, self.act
        if not last:
            dst = self.sc["xT"][:, :, tt * TT:(tt + 1) * TT].rearrange("kc p t -> p kc t")
            self.dma(pool, dst, xT[:, :, :], [xb], [self.xTb[tt]], xsem)
            return
        for s in range(4):
            for half in range(2):
                bk = self.nextbank()
                for k4 in range(4):
                    kc = half * 4 + k4
                    self.tr(self.ps[bk][:, k4 * 128:(k4 + 1) * 128], xT[:, kc, s * 128:(s + 1) * 128], self.ident[:, :],
                            [xb], [self.psb[bk]] if k4 in (0, 3) else [], inc=(k4 == 3))
                self.cp(act if half else dve, yout[:, s, half * 512:(half + 1) * 512], self.ps[bk][:, :],
                        [self.psb[bk]], [youtb])
        dst = self.y[tt * TT:(tt + 1) * TT, :].rearrange("(s p) c -> p s c", p=128)
        self.dma(pool, dst, yout[:, :, :], [youtb], [], youts)

    def norm_p1(self, xT, xb, sq, sqb):
        bk = self.nextbank()
        for kc in range(KC):
            i = kc % 2
            self.actf(sq[i][:, :], xT[:, kc, :], AF.Square, [xb], [sqb[i]])
            self.mm(self.ps[bk][:, :], self.onesb[:, :], sq[i][:, :], kc == 0, kc == KC - 1,
                    [sqb[i]], [self.psb[bk]] if kc in (0, KC - 1) else [], inc=True)
        return bk

    def norm_p2(self, bk, l, sub, seg, xT, xb, hT, hb, rstd, rsb, tmp, tmpb):
        dve = self.dve
        self.actf(rstd[:, :], self.ps[bk][:, :], AF.Ln, [self.psb[bk], self.cbuf], [rsb], bias=self.epsc[:, :], scale=1.0 / D)
        self.actf(rstd[:, :], rstd[:, :], AF.Exp, [rsb], [rsb], scale=-0.5)
        for kc in range(KC):
            i = kc % 2
            self.stt(dve, tmp[i][:, :], xT[:, kc, :], self.mcol(l, seg, sub, 0, kc), rstd[:, :], ALU.mult, ALU.mult,
                     [xb, rsb, self.mcbl[l]], [tmpb[i]])
            self.actf(hT[:, kc, :], tmp[i][:, :], AF.Identity, [tmpb[i], self.mcbl[l]], [hb],
                      bias=self.mcol(l, seg, sub, 1, kc), scale=1.0)

    def norm_mod(self, l, sub, seg, xT, xb, hT, hb, sq, sqb, rstd, rsb, tmp, tmpb):
        bk = self.norm_p1(xT, xb, sq, sqb)
        self.norm_p2(bk, l, sub, seg, xT, xb, hT, hb, rstd, rsb, tmp, tmpb)

    def ffn_phase(self, l, sub, first, last):
        sc = self.sc
        self.casts_now([(l, "wi", sub), (l, "wo", sub)])
        if l == 0 and sub == 2:
            self.casts_now([(1, "wi", 0), (1, "wo", 0), (1, "win"), (1, "wout"), (1, "wi", 2), (1, "wo", 2)])
        if l == 1:
            self.casts_now([k for k, _ in list(self.cast_rest)])
        pe, act, dve, pool, sp = self.pe, self.act, self.dve, self.pool, self.sp
        wi_d = sc[(l, "wi", sub)]
        wo_d = sc[(l, "wo", sub)]
        wib_, wob_ = self.scb[(l, "wi", sub)], self.scb[(l, "wo", sub)]
        wi_v = wi_d.rearrange("(kc p) n -> p kc n", p=128)
        wo_v = wo_d.rearrange("(fc p) m -> p fc m", p=128)
        with ExitStack() as st:
            sb = lambda n, s, t: self.sb(st, n, s, t)
            xT = [sb(f"xT{i}", [128, KC, TT], F32) for i in range(2)]
            xb = [Buf(f"xT{i}") for i in range(2)]
            xs = [self.dsem(f"xT{i}") for i in range(2)]
            xst = [self.dsem(f"xTst{i}") for i in range(2)]
            hTs = [sb(f"hT{i}", [128, KC, TT], BF16) for i in range(2)]; hbs = [Buf("hT0"), Buf("hT1")]
            sq = [sb(f"sq{i}", [128, TT], BF16) for i in range(2)]; sqb = [Buf(), Buf()]
            rstd = sb("rstd", [128, TT], F32); rsb = Buf("rstd")
            tmp = [sb(f"tmp{i}", [128, TT], F32) for i in range(2)]; tmpb = [Buf(), Buf()]
            actT = sb("actT", [128, FC, TT], BF16); ab = [Buf(f"actT{f}") for f in range(FC)]
            sg = [sb(f"sg{i}", [128, TT], F32) for i in range(2)]; sgb = [Buf(), Buf()]
            NWI, NWO = 3, (2 if (first or last) else 3)
            wi = [sb(f"wi{i}", [128, 2, KC, 256], BF16) for i in range(NWI)]
            wib = [Buf(f"wi{i}") for i in range(NWI)]
            wis = [self.dsem(f"wi{i}") for i in range(NWI)]
            wo = [sb(f"wo{i}", [128, FC, 256], BF16) for i in range(NWO)]
            wob = [Buf(f"wo{i}") for i in range(NWO)]
            wos = [self.dsem(f"wo{i}") for i in range(NWO)]
            xin = xinb = xins = yout = youtb = youts = None
            if first:
                xin = sb("xin", [128, 4, D], F32); xinb = Buf("xin"); xins = self.dsem("xin")
            if last:
                yout = sb("yout", [128, 4, D], F32); youtb = Buf("yout"); youts = self.dsem("yout")
            wic = 0
            woc = 0
            self.load_xT(0, xT[0], xb[0], xs[0], first, xin, xinb, xins)
            self.norm_mod(l, sub, 0, xT[0], xb[0], hTs[0], hbs[0], sq, sqb, rstd, rsb, tmp, tmpb)
            for tt in range(NTT):
                cur = tt % 2
                seg = tt // 4
                X, XB = xT[cur], xb[cur]
                hT, hb = hTs[cur], hbs[cur]
                for j in range(11):
                    slot = wic % NWI
                    wic += 1
                    for gu in range(2):
                        c0 = gu * DFF + j * 256
                        self.dma(sp, wi[slot][:, gu, :, :], wi_v[:, :, c0:c0 + 256], [wib_], [wib[slot]], wis[slot])
                    if j == 3 and tt + 1 < NTT:
                        self.load_xT(tt + 1, xT[1 - cur], xb[1 - cur], xs[1 - cur], first, xin, xinb, xins)
                    if j == 7 and tt + 1 < NTT:
                        nbk = self.norm_p1(xT[1 - cur], xb[1 - cur], sq, sqb)
                    if j == 8 and tt + 1 < NTT:
                        self.norm_p2(nbk, l, sub, (tt + 1) // 4, xT[1 - cur], xb[1 - cur], hTs[1 - cur], hbs[1 - cur],
                                     rstd, rsb, tmp, tmpb)
                    for fcl in range(2):
                        fc = 2 * j + fcl
                        bg = self.nextbank()
                        bu = self.nextbank()
                        for gu, bk in ((0, bg), (1, bu)):
                            for kc in range(KC):
                                self.mm(self.ps[bk][:, :], wi[slot][:, gu, kc, fcl * 128:(fcl + 1) * 128], hT[:, kc, :],
                                        kc == 0, kc == KC - 1, [wib[slot], hb],
                                        [self.psb[bk]] if kc in (0, KC - 1) else [], inc=(kc == KC - 1))
                        i = fc % 2
                        self.actf(sg[i][:, :], self.ps[bg][:, :], AF.Silu, [self.psb[bg]], [sgb[i]])
                        self.tt(dve, actT[:, fc, :], sg[i][:, :], self.ps[bu][:, :], ALU.mult,
                                [sgb[i], self.psb[bu]], [ab[fc]])
                for mb in range(4):
                    slot = woc % NWO
                    woc += 1
                    self.dma(sp, wo[slot][:, :, :], wo_v[:, :, mb * 256:(mb + 1) * 256], [wob_], [wob[slot]], wos[slot])
                    for ml in range(2):
                        mc = 2 * mb + ml
                        bk = self.nextbank()
                        for fc in range(FC):
                            self.mm(self.ps[bk][:, :], wo[slot][:, fc, ml * 128:(ml + 1) * 128], actT[:, fc, :],
                                    fc == 0, fc == FC - 1, [wob[slot], ab[fc]],
                                    [self.psb[bk]] if fc in (0, FC - 1) else [], inc=(fc == FC - 1))
                        self.stt(dve, X[:, mc, :], self.ps[bk][:, :], self.mcol(l, seg, sub, 2, mc), X[:, mc, :],
                                 ALU.mult, ALU.add, [self.psb[bk], self.mcbl[l]], [XB])
                self.store_xT(tt, X, XB, xst[cur], last, yout, youtb, youts)

    def mix_a(self, l, first):
        sc, d = self.sc, self.d
        pe, act, dve, pool, sp = self.pe, self.act, self.dve, self.pool, self.sp
        with ExitStack() as st:
            sb = lambda n, s, t: self.sb(st, n, s, t)
            xT = [sb(f"xT{i}", [128, KC, TT], F32) for i in range(2)]
            xb = [Buf(f"xT{i}") for i in range(2)]
            xs = [self.dsem(f"xT{i}") for i in range(2)]
            xss = [self.dsem(f"xTst{i}") for i in range(2)]
            hTs = [sb(f"hT{i}", [128, KC, TT], BF16) for i in range(2)]; hbs = [Buf("hT0"), Buf("hT1")]
            sq = [sb(f"sq{i}", [128, TT], BF16) for i in range(2)]; sqb = [Buf(), Buf()]
            rstd = sb("rstd", [128, TT], F32); rsb = Buf("rstd")
            tmp = [sb(f"tmp{i}", [128, TT], F32) for i in range(2)]; tmpb = [Buf(), Buf()]
            win = sb("win", [128, KC, 2048], BF16); winb = Buf("win"); wins = self.dsem("win")
            qsq = [sb(f"qsq{i}", [128, TT], BF16) for i in range(2)]; qsqb = [Buf(), Buf()]
            qrs = [sb(f"qrs{i}", [128, TT], F32) for i in range(2)]; qrsb = [Buf(), Buf()]
            NQ = 3
            qo = [sb(f"qo{i}", [128, TT], BF16) for i in range(NQ)]; qob = [Buf() for _ in range(NQ)]
            qos = [self.dsem(f"qo{i}") for i in range(NQ)]
            vst = [sb(f"vst{i}", [128, NH, 129], BF16) for i in range(2)]; vstb = [Buf(), Buf()]
            vss = [self.dsem(f"vst{i}") for i in range(2)]
            fT = sb("fT", [128, 4, TT], BF16); fTb = Buf("fT")
            xcs = [sb(f"xcs{i}", [128, 2, 512], BF16) for i in range(2)]; xcsb = [Buf(), Buf()]
            xcss = [self.dsem(f"xcs{i}") for i in range(2)]
            xcss2 = [self.dsem(f"xcsb{i}") for i in range(2)]
            xin = xinb = xins = None
            if first:
                xin = sb("xin", [128, 4, D], F32); xinb = Buf("xin"); xins = self.dsem("xin")
            self.casts_now([(l, "win"), (l, "wout")])
            self.load_xT(0, xT[0], xb[0], xs[0], first, xin, xinb, xins, store_scratch=first, st_sem=xss[0])
            wv = sc[(l, "win")].rearrange("(kc p) n -> p kc n", p=128)
            for h2 in range(2):
                self.dma(sp, win[:, :, h2 * 1024:(h2 + 1) * 1024], wv[:, :, h2 * 1024:(h2 + 1) * 1024],
                         [self.scb[(l, "win")]], [winb], wins)
            for i in range(2):
                pool.issue(lambda e, i=i: e.memset(vst[i][:, :, 128:129], 1.0), writes=[vstb[i]])
            qc = 0
            vc = 0
            xc = 0
            fTs = [fT, sb("fT1", [128, 4, TT], BF16)]
            fTbs = [fTb, Buf("fT1")]

            def xcs_section(t_, subs=(0, 1, 2, 3)):
                nonlocal xc
                fT_, fTb_ = fTs[t_ % 2], fTbs[t_ % 2]
                for s in subs:
                    xi = xc % 2
                    xc += 1
                    for ci, bdm in enumerate((self.bdc, self.bds)):
                        bx = self.nextbank()
                        for fi in range(4):
                            self.mm(self.ps[bx][:, fi * 128:(fi + 1) * 128], fT_[:, fi, s * 128:(s + 1) * 128], bdm[:, :],
                                    True, True, [fTb_, self.cbuf], [self.psb[bx]] if fi in (0, 3) else [], inc=(fi == 3),
                                    skip_group_check=True)
                        self.cp(dve, xcs[xi][:, ci, :], self.ps[bx][:, :], [self.psb[bx]], [xcsb[xi]])
                    r0 = t_ * TT + s * 128
                    self.dma(pool, sc["XC"][r0:r0 + 128, :], xcs[xi][:, 0, :], [xcsb[xi]], [self.scb["XC"]], xcss[xi])
                    self.dma(pool, sc["XS"][r0:r0 + 128, :], xcs[xi][:, 1, :], [xcsb[xi]], [self.scb["XS"]], xcss2[xi])

            self.norm_mod(l, 1, 0, xT[0], xb[0], hTs[0], hbs[0], sq, sqb, rstd, rsb, tmp, tmpb)
            for tt in range(NTT):
                cur = tt % 2
                seg = tt // 4
                X, XB = xT[cur], xb[cur]
                hT, hb = hTs[cur], hbs[cur]
                if tt + 1 < NTT:
                    self.load_xT(tt + 1, xT[1 - cur], xb[1 - cur], xs[1 - cur], first, xin, xinb, xins,
                                 store_scratch=first, st_sem=xss[1 - cur])

                def qk_finish(g, bq, i):
                    nonlocal qc
                    isk = g >= 4
                    h = g % 4
                    bs = self.nextbank()
                    self.mm(self.ps[bs][:, :], self.onesbd[:, :], qsq[i][:, :], True, True, [qsqb[i]], [self.psb[bs]], inc=True)
                    self.actf(qrs[i][:, :], self.ps[bs][:, :], AF.Ln, [self.psb[bs], self.cbuf], [qrsb[i]],
                              bias=self.epsc[:, :], scale=1.0 / HD)
                    self.actf(qrs[i][:, :], qrs[i][:, :], AF.Exp, [qrsb[i]], [qrsb[i]], scale=-0.5)
                    qi = qc % NQ
                    qc += 1
                    self.stt(dve, qo[qi][:, :], self.ps[bq][:, :], self.qkcol[l][:, (1 if isk else 0):(2 if isk else 1)],
                             qrs[i][:, :], ALU.mult, ALU.mult, [self.psb[bq], qrsb[i], self.qkb], [qob[qi]])
                    dst = sc["KT" if isk else "QT"][h, :, tt * TT:(tt + 1) * TT]
                    self.dma(pool, dst, qo[qi][:, :], [qob[qi]], [self.scb["KT" if isk else "QT"]], qos[qi])
                pend = None
                for g in range(8):
                    bq = self.nextbank()
                    for kc in range(KC):
                        self.mm(self.ps[bq][:, :], win[:, kc, g * 128:(g + 1) * 128], hT[:, kc, :], kc == 0, kc == KC - 1,
                                [winb, hb], [self.psb[bq]] if kc in (0, KC - 1) else [], inc=(kc == KC - 1))
                    i = g % 2
                    self.actf(qsq[i][:, :], self.ps[bq][:, :], AF.Square, [self.psb[bq]], [qsqb[i]])
                    if pend is not None:
                        qk_finish(*pend)
                    pend = (g, bq, i)
                    if tt > 0 and g % 2 == 1:
                        xcs_section(tt - 1, subs=(g // 2,))
                nbk = None
                if tt + 1 < NTT:
                    nbk = self.norm_p1(xT[1 - cur], xb[1 - cur], sq, sqb)
                qk_finish(*pend)
                if nbk is not None:
                    self.norm_p2(nbk, l, 1, (tt + 1) // 4, xT[1 - cur], xb[1 - cur], hTs[1 - cur], hbs[1 - cur],
                                 rstd, rsb, tmp, tmpb)
                for s in range(4):
                    bv = self.nextbank()
                    for kc in range(KC):
                        self.mm(self.ps[bv][:, :], hT[:, kc, s * 128:(s + 1) * 128], win[:, kc, 1024:1536], kc == 0, kc == KC - 1,
                                [winb, hb], [self.psb[bv]] if kc in (0, KC - 1) else [], inc=(kc == KC - 1))
                    vi = vc % 2
                    vc += 1
                    self.cp(dve, vst[vi][:, :, 0:128], self.ps[bv][:, :].rearrange("p (h d) -> p h d", h=NH),
                            [self.psb[bv]], [vstb[vi]])
                    r0 = tt * TT + s * 128
                    self.dma(pool, sc["V"][r0:r0 + 128, :], vst[vi][:, :, :].rearrange("p h d -> p (h d)"),
                             [vstb[vi]], [self.scb["V"]], vss[vi])
                fT, fTb = fTs[cur], fTbs[cur]
                for fi in range(4):
                    bf = self.nextbank()
                    for kc in range(KC):
                        self.mm(self.ps[bf][:, :], win[:, kc, 1536 + fi * 128:1536 + (fi + 1) * 128], hT[:, kc, :],
                                kc == 0, kc == KC - 1, [winb, hb], [self.psb[bf]] if kc in (0, KC - 1) else [],
                                inc=(kc == KC - 1))
                    self.cp(dve, fT[:, fi, :], self.ps[bf][:, :], [self.psb[bf]], [fTb])
            xcs_section(NTT - 1)

    def mix_d(self, l):
        sc, d = self.sc, self.d
        pe, act, dve, pool, sp = self.pe, self.act, self.dve, self.pool, self.sp
        with ExitStack() as st:
            sb = lambda n, s, t: self.sb(st, n, s, t)
            XC = sb("XC", [128, NKC, 512], BF16); XCb = Buf("XC")
            XS = sb("XS", [128, NKC, 512], BF16); XSb = Buf("XS")
            xls = self.dsem("xcl"); xls2 = self.dsem("xsl")
            NR = 3
            ring = [sb(f"dr{i}", [128, 4, 2, 512], BF16) for i in range(NR)]
            rb = [Buf(f"dr{i}") for i in range(NR)]
            rs = [self.dsem(f"dr{i}") for i in range(NR)]
            NO = 4
            fo = [sb(f"fo{i}", [128, 512], BF16) for i in range(NO)]
            fob = [Buf() for _ in range(NO)]
            fos = [self.dsem(f"fo{i}") for i in range(NO)]
            for half in range(2):
                r = slice(half * 16, (half + 1) * 16)
                self.dma(sp, XC[:, r, :], sc["XC"][half * 2048:(half + 1) * 2048, :].rearrange("(k p) f -> p k f", p=128),
                         [self.scb["XC"]], [XCb], xls)
                self.dma(sp, XS[:, r, :], sc["XS"][half * 2048:(half + 1) * 2048, :].rearrange("(k p) f -> p k f", p=128),
                         [self.scb["XS"]], [XSb], xls2)
            rc = 0
            oc = 0
            for j in range(8):
                banks = [self.nextbank() for _ in range(4)]
                for ig in range(8):
                    slot = rc % NR
                    rc += 1
                    self.dma(sp, ring[slot][:, :, :, :].rearrange("p a b c -> p (a b c)"), d["dftm"][j * 8 + ig, :, :],
                             [], [rb[slot]], rs[slot])
                    for i4 in range(4):
                        k = ig * 4 + i4
                        for ci, (XM, XMb) in enumerate(((XC, XCb), (XS, XSb))):
                            for fi in range(4):
                                st_ = (k == 0 and ci == 0)
                                sp_ = (k == NKC - 1 and ci == 1)
                                self.mm(self.ps[banks[fi]][:, :], XM[:, k, fi * 128:(fi + 1) * 128], ring[slot][:, i4, ci, :],
                                        st_, sp_, [XMb, rb[slot]], [self.psb[banks[fi]]] if (st_ or sp_) else [],
                                        inc=(sp_ or (i4 == 3 and ci == 1 and fi == 3)))
                for fi in range(4):
                    oi = oc % NO
                    oc += 1
                    self.cp(act if fi % 2 else dve, fo[oi][:, :], self.ps[banks[fi]][:, :], [self.psb[banks[fi]]], [fob[oi]])
                    self.dma(pool, sc["FOT"][fi, :, j * 512:(j + 1) * 512], fo[oi][:, :], [fob[oi]], [self.scb["FOT"]], fos[oi])

    def mix_c(self, l, last):
        sc, d = self.sc, self.d
        pe, act, dve, pool, sp = self.pe, self.act, self.dve, self.pool, self.sp
        li = 0.8 - 0.6 * math.exp(-0.3 * l)
        with ExitStack() as st:
            sb = lambda n, s, t: self.sb(st, n, s, t)
            KT = sb("KT", [128, NH, NT], BF16); KTbs = [Buf(f"KT{h}") for h in range(NH)]
            ktss = [self.dsem(f"KT{h}") for h in range(NH)]
            V = sb("V", [128, NKC, NH * 129], BF16); Vb = Buf("V"); vs = self.dsem("V")
            wout = sb("wout", [128, KC, D], BF16); woutb = Buf("wout"); wouts = self.dsem("wout")
            QT = [sb(f"QT{i}", [128, NH, TT], BF16) for i in range(2)]; QTb = [Buf(), Buf()]
            qts = [self.dsem(f"QT{i}") for i in range(2)]
            foT = [sb(f"foT{i}", [128, 4, TT], BF16) for i in range(2)]; foTb = [Buf(), Buf()]
            fots = [self.dsem(f"foT{i}") for i in range(2)]
            xT = sb("xT", [128, KC, TT], F32); xb = Buf("xT"); xs = self.dsem("xT"); xst = self.dsem("xTst0")
            attnTs = [sb(f"attnT{i}", [128, NH, TT], BF16) for i in range(2)]; atbs = [Buf("attnT0"), Buf("attnT1")]
            NP = 3
            Pm = [[sb(f"P{j}_{i}", [128, TT], BF16) for i in range(NP)] for j in range(2)]
            Pb = [[Buf() for _ in range(NP)] for _ in range(2)]
            mt = [sb(f"mt{i}", [128, TT], F32) for i in range(2)]; mtb = [Buf(), Buf()]
            t2 = [sb(f"t2{i}", [128, 128], F32) for i in range(2)]; t2b = [Buf(), Buf()]
            yout = youtb = youts = None
            if last:
                yout = sb("yout", [128, 4, D], F32); youtb = Buf("yout"); youts = self.dsem("yout")
            cast_plan = {}
            if l == 0:
                cast_plan = {40: (0, "wi", 2), 200: (0, "wo", 2), 500: (1, "wi", 0), 650: (1, "wo", 0),
                             800: (1, "win"), 850: (1, "wout")}
            def load_resident():
                self.dma(sp, KT[:, 0, :], sc["KT"][0, :, :], [self.scb["KT"]], [KTbs[0]], ktss[0])
                for half in range(2):
                    self.dma(sp, V[:, half * 16:(half + 1) * 16, :],
                             sc["V"][half * 2048:(half + 1) * 2048, :].rearrange("(k p) f -> p k f", p=128),
                             [self.scb["V"]], [Vb], vs)
                for h in range(1, NH):
                    self.dma(sp, KT[:, h, :], sc["KT"][h, :, :], [self.scb["KT"]], [KTbs[h]], ktss[h])
                self.dma(sp, wout[:, :, :], sc[(l, "wout")].rearrange("(kc p) n -> p kc n", p=128),
                         [self.scb[(l, "wout")]], [woutb], wouts)

            def load_q(qt, i):
                self.dma(sp, QT[i][:, :, :], sc["QT"][:, :, qt * TT:(qt + 1) * TT].rearrange("h p t -> p h t"),
                         [self.scb["QT"]], [QTb[i]], qts[i])

            def load_fo(qt, i):
                self.dma(sp, foT[i][:, :, :], sc["FOT"][:, :, qt * TT:(qt + 1) * TT].rearrange("f p t -> p f t"),
                         [self.scb["FOT"]], [foTb[i]], fots[i])
            load_q(0, 0)
            load_resident()
            load_fo(0, 0)
            load_fo(1, 1)
            SB = [[0, 1], [2, 3]]
            OB = [4, 5, 6]
            MISC = 7

            def oacc(j, s):
                idx = j * 4 + s
                return OB[idx // 3], (idx % 3) * 129
            ocp = sb("ocp", [128, 8 * 129], F32); ocb = Buf("ocp")
            its = [(qt, h, kc) for qt in range(NTT) for h in range(NH) for kc in range(NKC)]
            NIT = len(its)

            def emit_qk(idx):
                qt, h, kc = its[idx]
                sl = idx % 2
                cur = qt % 2
                for j in range(2):
                    bk = SB[sl][j]
                    self.mm(self.ps[bk][:, :], KT[64 * j:64 * (j + 1), h, kc * 128:(kc + 1) * 128],
                            QT[cur][64 * j:64 * (j + 1), h, :], True, True, [KTbs[h], QTb[cur]], [self.psb[bk]], inc=True)

            def emit_exp(idx):
                qt, h, kc = its[idx]
                sl = idx % 2
                pi = idx % NP
                dd = kc - 4 * qt
                mixed = (-1 <= dd <= 4)
                for j in range(2):
                    bk = SB[sl][j]
                    hj = h * 2 + j
                    col = self.table[:, hj, kc * NTT + qt:kc * NTT + qt + 1]
                    if mixed:
                        c0 = (4 - dd) * 128
                        self.actf(mt[j][:, :], self.ps[bk][:, :], AF.Exp, [self.psb[bk], self.tableb], [mtb[j]],
                                  bias=col, scale=0.125)
                        self.tt(dve, Pm[j][pi][:, :], mt[j][:, :], self.strips[:, hj, c0:c0 + TT], ALU.mult,
                                [mtb[j], self.stripb], [Pb[j][pi]])
                    else:
                        self.actf(Pm[j][pi][:, :], self.ps[bk][:, :], AF.Exp, [self.psb[bk], self.tableb],
                                  [Pb[j][pi]], bias=col, scale=0.125)

            def emit_av(idx):
                qt, h, kc = its[idx]
                pi = idx % NP
                for j in range(2):
                    for s in range(4):
                        ob, oc0 = oacc(j, s)
                        i8 = j * 4 + s
                        first_in_bank = (kc == 0 and i8 % 3 == 0)
                        lastk = (kc == NKC - 1)
                        self.mm(self.ps[ob][:, oc0:oc0 + 129], Pm[j][pi][:, s * 128:(s + 1) * 128],
                                V[:, kc, h * 129:(h + 1) * 129], first_in_bank, lastk,
                                [Pb[j][pi], Vb], [self.psb[ob]] if (first_in_bank or lastk) else [],
                                inc=(lastk or (j == 1 and s == 3)), skip_group_check=True)

            rr8 = sb("rr8", [128, 8], F32); rr8b = Buf("rr8")
            oo4 = [sb("oo4_0", [128, 4, 128], F32)] * 2; oo4b = [Buf()] * 2
            jk4 = sb("jk4", [128, 4, 128], F32); jk4b = Buf("jk4")
            ss4 = [sb(f"ss4_{i}", [128, 4], F32) for i in range(2)]; ss4b = [Buf(), Buf()]
            on4 = [sb("on4_0", [128, 4, 128], BF16)] * 2; on4b = [Buf()] * 2

            def emit_epilogue(qt, h, ec0):
                e = h % 2
                ocv = ocp[:, :].rearrange("p (a c) -> p a c", c=129)
                self.cp(dve, ocp[:, 0:387], self.ps[OB[0]][:, 0:387], [self.psb[OB[0]]], [ocb])
                self.cp(dve, ocp[:, 387:774], self.ps[OB[1]][:, 0:387], [self.psb[OB[1]]], [ocb])
                self.cp(dve, ocp[:, 774:1032], self.ps[OB[2]][:, 0:258], [self.psb[OB[2]]], [ocb])
                dve.issue(lambda e_: e_.reciprocal(out=rr8[:, :], in_=ocv[:, :, 128]), reads=[ocb], writes=[rr8b])
                for s in range(4):
                    i2 = s % 2
                    self.ts(dve, t2[i2][:, :], ocv[:, 4 + s, 0:128], rr8[:, 4 + s:5 + s], self.qkcol[l][:, 3:4],
                            ALU.mult, ALU.mult, [ocb, rr8b, self.qkb], [t2b[i2]])
                    self.stt(dve, oo4[e][:, s, :], ocv[:, s, 0:128], rr8[:, s:s + 1], t2[i2][:, :],
                             ALU.mult, ALU.add, [ocb, rr8b, t2b[i2]], [oo4b[e]])
                self.tt(dve, jk4[:, :, :], oo4[e][:, :, :], oo4[e][:, :, :], ALU.mult, [oo4b[e]], [jk4b])
                dve.issue(lambda e_, e=e: e_.reduce_sum(out=ss4[e][:, :], in_=jk4[:, :, :], axis=mybir.AxisListType.X),
                          reads=[jk4b], writes=[ss4b[e]])
                self.ts(dve, ss4[e][:, :], ss4[e][:, :], 1.0 / 128, EPS, ALU.mult, ALU.add, [ss4b[e]], [ss4b[e]])

                def part_a2(e=e):
                    self.actf(ss4[e][:, :], ss4[e][:, :], AF.Ln, [ss4b[e]], [ss4b[e]])
                    self.actf(ss4[e][:, :], ss4[e][:, :], AF.Exp, [ss4b[e]], [ss4b[e]], scale=-0.5)
                    for s in range(4):
                        self.ts(dve, on4[e][:, s, :], oo4[e][:, s, :], ss4[e][:, s:s + 1], None, ALU.mult, None,
                                [oo4b[e], ss4b[e]], [on4b[e]])

                def part_b(e=e, h=h, qt=qt):
                    pv = self.ps[MISC][:, :].bitcast(BF16)
                    for s in range(4):
                        self.tr(pv[:, s * 128:(s + 1) * 128], on4[e][:, s, :], self.identb[:, :], [on4b[e], self.cbuf],
                                [self.psb[MISC]] if s in (0, 3) else [], inc=(s == 3))
                    self.ts(dve, attnTs[qt % 2][:, h, :], pv[:, 0:512], self.qkcol[l][:, 2:3], None, ALU.mult, None,
                            [self.psb[MISC], self.qkb], [atbs[qt % 2]])
                return part_a2, part_b

            ec = 0
            adag = None
            if l == 0 and any(pl == 1 for pl, _ in self.phases):
                adag = self.ada_gen(1, st, 128, MISC, nslots=2)
            from collections import deque
            dq = deque()

            def wout_group(qt, mc):
                cur = qt % 2
                seg = qt // 4

                def f():
                    if mc == 0:
                        self.load_xT(qt, xT, xb, xs, False)
                    bk = MISC
                    for jc in range(KC):
                        if jc < 4:
                            rhs, rbuf = attnTs[cur][:, jc, :], atbs[cur]
                        else:
                            rhs, rbuf = foT[cur][:, jc - 4, :], foTb[cur]
                        self.mm(self.ps[bk][:, :], wout[:, jc, mc * 128:(mc + 1) * 128], rhs, jc == 0, jc == KC - 1,
                                [woutb, rbuf], [self.psb[bk]] if jc in (0, KC - 1) else [], inc=(jc == KC - 1))
                    self.stt(dve, xT[:, mc, :], self.ps[bk][:, :], self.mcol(l, seg, 1, 2, mc), xT[:, mc, :],
                             ALU.mult, ALU.add, [self.psb[bk], self.mcbl[l]], [xb])
                    if mc == KC - 1:
                        self.store_xT(qt, xT, xb, xst, last, yout, youtb, youts)
                        if qt + 2 < NTT:
                            load_fo(qt + 2, cur)
                return f

            emit_qk(0)
            emit_qk(1)
            for idx in range(NIT):
                qt, h, kc = its[idx]
                cur = qt % 2
                if h == 0 and kc == 0 and qt + 1 < NTT:
                    load_q(qt + 1, 1 - cur)
                if idx in cast_plan:
                    self.casts_now([cast_plan[idx]])
                emit_exp(idx)
                if idx + 2 < NIT:
                    emit_qk(idx + 2)
                emit_av(idx)
                if adag is not None and idx % 6 == 3:
                    if next(adag, "done") == "done":
                        adag = None
                elif dq and dq[0][0] <= idx:
                    dq.popleft()[1]()
                if kc == NKC - 1:
                    pa2, pb = emit_epilogue(qt, h, ec)
                    dq.append((idx + 7, pa2))
                    dq.append((idx + 10, pb))
                    if h == NH - 1:
                        for mc in range(KC):
                            dq.append((idx + 12 + 5 * mc, wout_group(qt, mc)))
            while dq:
                dq.popleft()[1]()
            if adag is not None:
                for _ in adag:
                    pass


FULL_PHASES = [(0, "f1"), (0, "mix"), (0, "f2"), (1, "f1"), (1, "mix"), (1, "f2")]
_NC_CACHE = {}


def _get_nc(phases):
    key = tuple(phases)
    if key not in _NC_CACHE:
        _NC_CACHE[key] = Builder(list(phases)).build()
    return _NC_CACHE[key]


def _run(inputs, phases):
    c = _host_consts()
    f32 = lambda a: np.ascontiguousarray(np.asarray(a, dtype=np.float32))
    xs = f32(inputs["x_sample"])
    xp = f32(inputs["x_prompt"])
    cs_ = f32(inputs["c_sample"])
    cp_ = f32(inputs["c_prompt"])
    shared = {
        "ada_w": f32(inputs["ada_w"]),
        "ada_b": f32(inputs["ada_b"]).reshape(2, 72, 128),
        "norm_ffn1": f32(inputs["norm_ffn1"]).reshape(2, 8, 128),
        "norm_mix": f32(inputs["norm_mix"]).reshape(2, 8, 128),
        "norm_ffn2": f32(inputs["norm_ffn2"]).reshape(2, 8, 128),
        "ffn1_wi": f32(inputs["ffn1_wi"]), "ffn1_wo": f32(inputs["ffn1_wo"]),
        "ffn2_wi": f32(inputs["ffn2_wi"]), "ffn2_wo": f32(inputs["ffn2_wo"]),
        "w_in": f32(inputs["w_in"]), "w_out": f32(inputs["w_out"]),
        "q_norm": f32(inputs["q_norm"]).reshape(2, 1, 64),
        "k_norm": f32(inputs["k_norm"]).reshape(2, 1, 64),
        "lambda_qk": f32(inputs["lambda_qk"]).reshape(2, 1, 256),
        "subln": f32(inputs["subln"]).reshape(2, 1, 128),
        "rel_bias": f32(inputs["rel_bias"]).reshape(32, 8),
    }
    for k in ("ident", "identb", "jmat", "onehot", "sel15", "sel31", "selL", "selR", "onesbd", "bdc", "bds"):
        shared[k] = c[k]
    in_maps = []
    for i in range(NCORES):
        m = dict(shared)
        if i < 4:
            m["x"] = xs[i]
            m["c2"] = np.ascontiguousarray(np.stack([cs_[i], cs_[i]]).reshape(16, 128))
            m["mask"] = c["mask_s"]
            m["dftm"] = c["dft_s"]
        else:
            p = 2 * (i - 4)
            m["x"] = np.ascontiguousarray(xp[p:p + 2].reshape(NT, D))
            m["c2"] = np.ascontiguousarray(cp_[p:p + 2].reshape(16, 128))
            m["mask"] = c["mask_p"]
            m["dftm"] = c["dft_p"]
        in_maps.append(m)
    nc = _get_nc(phases)
    res = run_bass_kernel_spmd(nc, in_maps, core_ids=list(range(NCORES)))
    ys = [np.asarray(r["y"], dtype=np.float32) for r in res.results]
    y_sample = np.stack(ys[0:4]).reshape(4, 4096, D)
    y_prompt = np.stack([ys[4 + i].reshape(2, 2048, D) for i in range(4)]).reshape(8, 2048, D)
    return (y_prompt, y_sample)


def kernel(**inputs):
    return _run(inputs, FULL_PHASES)
```

```python
import math
from contextlib import ExitStack

import numpy as np
import ml_dtypes

import concourse.bass as bass
import concourse.mybir as mybir
from concourse.bass_utils import run_bass_kernel_spmd

F32 = mybir.dt.float32
BF16 = mybir.dt.bfloat16
AF = mybir.ActivationFunctionType
ALU = mybir.AluOpType

NCORES = 8
NT = 4096
D = 1024
KC = 8
DFF = 2816
FC = 22
TT = 512
NTT = NT // TT
NKC = NT // 128
EPS = 1e-6
NEG = -30000.0
HD = 64
NH = 4


class Sem:
    __slots__ = ("h", "val", "name")

    def __init__(self, h, name):
        self.h = h
        self.val = 0
        self.name = name


class Buf:
    __slots__ = ("name", "w", "rs")

    def __init__(self, name=""):
        self.name = name
        self.w = None
        self.rs = {}


class Eng:
    def __init__(self, name, sem):
        self.name = name
        self.sem = sem
        self.ops = []
        self.waited = {}

    def issue(self, fn, reads=(), writes=(), inc=True, dsem=None, extra=()):
        need = {}

        def add(ev):
            if ev is None:
                return
            s, v = ev
            if v > need.get(s, 0):
                need[s] = v
        for b in reads:
            add(b.w)
        for b in writes:
            add(b.w)
            for s, v in b.rs.items():
                add((s, v))
        for ev in extra:
            add(ev)
        waits = []
        for s, v in need.items():
            if self.name == "pe" and s is self.sem:
                continue
            if self.waited.get(s, 0) < v:
                self.waited[s] = v
                waits.append((s, v))
        if dsem is not None:
            dsem.val += 16
            ev = (dsem, dsem.val)
            do_inc = (dsem, 16)
        elif inc:
            self.sem.val += 1
            ev = (self.sem, self.sem.val)
            do_inc = (self.sem, 1)
        else:
            ev = (self.sem, self.sem.val + 1)
            do_inc = None
        self.ops.append((waits, fn, do_inc))
        s, v = ev
        for b in reads:
            if b.rs.get(s, 0) < v:
                b.rs[s] = v
        for b in writes:
            b.w = ev
            b.rs = {}
        return ev

    def wait_event(self, ev):
        if ev is None:
            return
        s, v = ev
        if self.name == "pe" and s is self.sem:
            return
        if self.waited.get(s, 0) < v:
            self.waited[s] = v
            self.ops.append(([(s, v)], None, None))

    def replay(self, e):
        for waits, fn, do_inc in self.ops:
            for s, v in waits:
                e.wait_ge(s.h, v)
            if fn is None:
                continue
            ins = fn(e)
            if do_inc is not None:
                ins.then_inc(do_inc[0].h, do_inc[1])


def _rel_bucket_np(rel):
    try:
        import jax
        import jax.numpy as jnp
        cpu = jax.devices("cpu")[0]
        with jax.default_device(cpu):
            rel_j = jnp.asarray(rel, dtype=jnp.int32)
            nb = 16
            max_exact = 8
            ret = (rel_j > 0).astype(jnp.int32) * nb
            n = jnp.abs(rel_j)
            nf = jnp.maximum(n, 1).astype(jnp.float32)
            large = max_exact + (jnp.log(nf / max_exact) / math.log(128 / max_exact)
                                 * (nb - max_exact)).astype(jnp.int32)
            large = jnp.minimum(large, nb - 1)
            out = ret + jnp.where(n < max_exact, n, large)
            return np.asarray(out)
    except Exception:
        rel = np.asarray(rel, dtype=np.int32)
        nb, max_exact = 16, 8
        ret = (rel > 0).astype(np.int32) * nb
        n = np.abs(rel)
        nf = np.maximum(n, 1).astype(np.float32)
        large = max_exact + (np.log(nf / np.float32(max_exact)) / np.float32(math.log(128 / max_exact))
                             * np.float32(nb - max_exact)).astype(np.int32)
        large = np.minimum(large, nb - 1)
        return ret + np.where(n < max_exact, n, large)


_CONST_CACHE = {}


def _host_consts():
    if "c" in _CONST_CACHE:
        return _CONST_CACHE["c"]
    c = {}
    c["ident"] = np.eye(128, dtype=np.float32)
    c["identb"] = np.eye(128, dtype=np.float32).astype(ml_dtypes.bfloat16)
    c["jmat"] = np.ascontiguousarray(np.eye(128, dtype=np.float32)[::-1])
    y = np.arange(1280)
    bk = _rel_bucket_np(639 - y)
    oh = np.zeros((32, 1280), np.float32)
    oh[bk, y] = 1.0
    c["onehot"] = oh
    s15 = np.zeros((32, 128), np.float32); s15[15] = 1.0
    s31 = np.zeros((32, 128), np.float32); s31[31] = 1.0
    c["sel15"] = s15
    c["sel31"] = s31
    selL = np.zeros((NKC, NTT), np.float32)
    selR = np.zeros((NKC, NTT), np.float32)
    for kc in range(NKC):
        for qt in range(NTT):
            d = kc - 4 * qt
            if d <= -2:
                selL[kc, qt] = 1.0
            elif d >= 5:
                selR[kc, qt] = 1.0
    c["selL"] = np.ascontiguousarray(np.broadcast_to(selL.reshape(1, -1), (128, NKC * NTT)))
    c["selR"] = np.ascontiguousarray(np.broadcast_to(selR.reshape(1, -1), (128, NKC * NTT)))
    m_s = np.zeros((NKC, NTT), np.float32)
    m_p = np.zeros((NKC, NTT), np.float32)
    for kc in range(NKC):
        for qt in range(NTT):
            if (kc // 16) != (qt // 4):
                m_p[kc, qt] = NEG
    c["mask_s"] = np.ascontiguousarray(np.broadcast_to(m_s.reshape(1, -1), (128, NKC * NTT)))
    c["mask_p"] = np.ascontiguousarray(np.broadcast_to(m_p.reshape(1, -1), (128, NKC * NTT)))
    bd = np.zeros((128, 128), np.float32)
    bd[:64, :64] = 1.0
    bd[64:, 64:] = 1.0
    c["onesbd"] = bd.astype(ml_dtypes.bfloat16)
    a = np.arange(64)
    ang = 2.0 * np.pi * np.outer(a, a) / 64.0
    c64 = np.cos(ang) / 8.0
    s64 = np.sin(ang) / 8.0
    bdc = np.zeros((128, 128)); bds = np.zeros((128, 128))
    bdc[:64, :64] = c64; bdc[64:, 64:] = c64
    bds[:64, :64] = -s64; bds[64:, 64:] = -s64
    c["bdc"] = bdc.astype(np.float32).astype(ml_dtypes.bfloat16)
    c["bds"] = bds.astype(np.float32).astype(ml_dtypes.bfloat16)

    def dft_pack(S):
        nseq = NT // S
        s = np.arange(S, dtype=np.int64)
        m = (np.outer(s, s) % S).astype(np.float64)
        cs = (np.cos(2 * np.pi * m / S) / math.sqrt(S)).astype(np.float32)
        sn = (np.sin(2 * np.pi * m / S) / math.sqrt(S)).astype(np.float32)
        C = np.zeros((NT, NT), np.float32)
        Sn = np.zeros((NT, NT), np.float32)
        for q in range(nseq):
            C[q * S:(q + 1) * S, q * S:(q + 1) * S] = cs
            Sn[q * S:(q + 1) * S, q * S:(q + 1) * S] = sn
        out = np.zeros((8, 8, 128, 4, 2, 512), ml_dtypes.bfloat16)
        for mi, M in enumerate((C, Sn)):
            M6 = M.reshape(8, 4, 128, 8, 512)
            out[:, :, :, :, mi, :] = M6.transpose(3, 0, 2, 1, 4).astype(ml_dtypes.bfloat16)
        return np.ascontiguousarray(out.reshape(64, 128, 4 * 2 * 512))
    c["dft_s"] = dft_pack(4096)
    c["dft_p"] = dft_pack(2048)
    _CONST_CACHE["c"] = c
    return c


class Builder:
    def __init__(self, phases):
        self.phases = phases
        self.nc = bass.Bass("TRN2", target_bir_lowering=False)
        self.gst = ExitStack()
        self.nsem = 0
        self.dma_sems = []
        self.bank_rr = 0

    def sem(self, name):
        self.nsem += 1
        return Sem(self.gst.enter_context(self.nc.semaphore(f"s{self.nsem}_{name}")), name)

    def dsem(self, name, barrier=True):
        if not hasattr(self, "semcache"):
            self.semcache = {}
        if name in self.semcache:
            return self.semcache[name]
        s = self.sem(name)
        if barrier:
            self.dma_sems.append(s)
        else:
            self.nobar_sems = getattr(self, "nobar_sems", []) + [s]
        self.semcache[name] = s
        return s

    def din(self, name, shape, dt):
        return self.nc.dram_tensor(name, list(shape), dt, kind="ExternalInput").ap()

    def dscr(self, name, shape, dt):
        return self.nc.dram_tensor(name, list(shape), dt).ap()

    def sb(self, st, name, shape, dt):
        self.nsb = getattr(self, "nsb", 0) + 1
        return st.enter_context(self.nc.sbuf_tensor(f"{name}_{self.nsb}", list(shape), dt))

    def nextbank(self):
        b = self.bank_rr
        self.bank_rr = (b + 1) % 8
        return b

    def mm(self, out, lhsT, rhs, start, stop, reads, writes=(), inc=False, **kw):
        return self.pe.issue(lambda e: e.matmul(out, lhsT=lhsT, rhs=rhs, start=start, stop=stop, **kw),
                             reads=reads, writes=writes, inc=inc)

    def tr(self, out, in_, ident, reads, writes=(), inc=True):
        return self.pe.issue(lambda e: e.transpose(out=out, in_=in_, identity=ident),
                             reads=reads, writes=writes, inc=inc)

    def actf(self, out, in_, func, reads, writes, bias=None, scale=None, accum_out=None):
        kw = {}
        if bias is not None:
            kw["bias"] = bias
        if scale is not None:
            kw["scale"] = scale
        if accum_out is not None:
            kw["accum_out"] = accum_out
        return self.act.issue(lambda e: e.activation(out=out, in_=in_, func=func, **kw),
                              reads=reads, writes=writes)

    def ts(self, eng, out, in0, s1, s2, op0, op1, reads, writes):
        if op1 is None:
            return eng.issue(lambda e: e.tensor_scalar(out=out, in0=in0, scalar1=s1, scalar2=None, op0=op0),
                             reads=reads, writes=writes)
        return eng.issue(lambda e: e.tensor_scalar(out=out, in0=in0, scalar1=s1, scalar2=s2, op0=op0, op1=op1),
                         reads=reads, writes=writes)

    def stt(self, eng, out, in0, scalar, in1, op0, op1, reads, writes):
        return eng.issue(lambda e: e.scalar_tensor_tensor(out=out, in0=in0, scalar=scalar, in1=in1, op0=op0, op1=op1),
                         reads=reads, writes=writes)

    def tt(self, eng, out, in0, in1, op, reads, writes):
        return eng.issue(lambda e: e.tensor_tensor(out=out, in0=in0, in1=in1, op=op), reads=reads, writes=writes)

    def cp(self, eng, out, in_, reads, writes):
        if eng is self.act:
            return eng.issue(lambda e: e.activation(out=out, in_=in_, func=AF.Identity), reads=reads, writes=writes)
        return eng.issue(lambda e: e.tensor_copy(out=out, in_=in_), reads=reads, writes=writes)

    def dma(self, eng, out, in_, reads, writes, dsem):
        return eng.issue(lambda e: e.dma_start(out=out, in_=in_), reads=reads, writes=writes, dsem=dsem)

    def barrier(self):
        evs = []
        for e in self.engs:
            if e.sem.val > 0:
                evs.append((e.sem, e.sem.val))
        for s in self.dma_sems:
            if s.val > 0:
                evs.append((s, s.val))
        for e in self.engs:
            for ev in evs:
                e.wait_event(ev)

    def build(self):
        nc = self.nc
        g = self.gst
        with g:
            self._build()
        return nc

    def _build(self):
        nc = self.nc
        d = {}
        d["x"] = self.din("x", [NT, D], F32)
        d["c2"] = self.din("c2", [16, 128], F32)
        d["ada_w"] = self.din("ada_w", [2, D, 9 * D], F32)
        d["ada_b"] = self.din("ada_b", [2, 72, 128], F32)
        for n in ("norm_ffn1", "norm_mix", "norm_ffn2"):
            d[n] = self.din(n, [2, 8, 128], F32)
        d["ffn1_wi"] = self.din("ffn1_wi", [2, D, 2 * DFF], F32)
        d["ffn1_wo"] = self.din("ffn1_wo", [2, DFF, D], F32)
        d["ffn2_wi"] = self.din("ffn2_wi", [2, D, 2 * DFF], F32)
        d["ffn2_wo"] = self.din("ffn2_wo", [2, DFF, D], F32)
        d["w_in"] = self.din("w_in", [2, D, 2048], F32)
        d["w_out"] = self.din("w_out", [2, D, D], F32)
        d["q_norm"] = self.din("q_norm", [2, 1, 64], F32)
        d["k_norm"] = self.din("k_norm", [2, 1, 64], F32)
        d["lambda_qk"] = self.din("lambda_qk", [2, 1, 256], F32)
        d["subln"] = self.din("subln", [2, 1, 128], F32)
        d["rel_bias"] = self.din("rel_bias", [32, 8], F32)
        d["ident"] = self.din("ident", [128, 128], F32)
        d["identb"] = self.din("identb", [128, 128], BF16)
        d["jmat"] = self.din("jmat", [128, 128], F32)
        d["onehot"] = self.din("onehot", [32, 1280], F32)
        d["sel15"] = self.din("sel15", [32, 128], F32)
        d["sel31"] = self.din("sel31", [32, 128], F32)
        d["selL"] = self.din("selL", [128, 256], F32)
        d["selR"] = self.din("selR", [128, 256], F32)
        d["mask"] = self.din("mask", [128, 256], F32)
        d["onesbd"] = self.din("onesbd", [128, 128], BF16)
        d["bdc"] = self.din("bdc", [128, 128], BF16)
        d["bds"] = self.din("bds", [128, 128], BF16)
        d["dftm"] = self.din("dftm", [64, 128, 4096], BF16)
        self.y = nc.dram_tensor("y", [NT, D], F32, kind="ExternalOutput").ap()
        self.d = d
        sc = {}
        for l in range(2):
            sc[(l, "wi", 0)] = self.dscr(f"wi1b{l}", [D, 2 * DFF], BF16)
            sc[(l, "wo", 0)] = self.dscr(f"wo1b{l}", [DFF, D], BF16)
            sc[(l, "wi", 2)] = self.dscr(f"wi2b{l}", [D, 2 * DFF], BF16)
            sc[(l, "wo", 2)] = self.dscr(f"wo2b{l}", [DFF, D], BF16)
            sc[(l, "win")] = self.dscr(f"winb{l}", [D, 2048], BF16)
            sc[(l, "wout")] = self.dscr(f"woutb{l}", [D, D], BF16)
        sc["xT"] = self.dscr("xTs", [KC, 128, NT], F32)
        sc["QT"] = self.dscr("QTs", [NH, 128, NT], BF16)
        sc["KT"] = self.dscr("KTs", [NH, 128, NT], BF16)
        sc["V"] = self.dscr("Vs", [NT, NH * 129], BF16)
        sc["XC"] = self.dscr("XCs", [NT, 512], BF16)
        sc["XS"] = self.dscr("XSs", [NT, 512], BF16)
        sc["FOT"] = self.dscr("FOTs", [4, 128, NT], BF16)
        sc["G"] = self.dscr("Gs", [8, 1280], F32)
        self.sc = sc
        self.scb = {k: Buf(str(k)) for k in sc}
        self.xTb = [Buf(f"xTs{t}") for t in range(NTT)]

        self.pe = Eng("pe", self.sem("pe"))
        self.act = Eng("act", self.sem("act"))
        self.dve = Eng("dve", self.sem("dve"))
        self.pool = Eng("pool", self.sem("pool"))
        self.sp = Eng("sp", self.sem("sp"))
        self.engs = [self.pe, self.act, self.dve, self.pool, self.sp]

        self.ps = [self.gst.enter_context(nc.psum_tensor(f"ps{b}", [128, 512], F32)) for b in range(8)]
        self.psb = [Buf(f"ps{b}") for b in range(8)]

        P = self.gst
        self.ident = self.sb(P, "ident", [128, 128], F32)
        self.identb = self.sb(P, "identb", [128, 128], BF16)
        self.onesb = self.sb(P, "onesb", [128, 128], BF16)
        self.onesbd = self.sb(P, "onesbd", [128, 128], BF16)
        self.bdc = self.sb(P, "bdc", [128, 128], BF16)
        self.bds = self.sb(P, "bds", [128, 128], BF16)
        self.strips = self.sb(P, "strips", [128, 8, 1152], F32)
        self.table = self.sb(P, "table", [128, 8, 256], F32)
        self.modc = [self.sb(P, f"modc{l}", [128, 2 * 3 * 3 * 8], F32) for l in range(2)]
        self.qkcol = [self.sb(P, f"qkcol{l}", [128, 4], F32) for l in range(2)]
        self.cbuf = Buf("consts")
        self.epsc = self.sb(P, "epsc", [128, 1], F32)
        self.colsL = [self.sb(P, f"colsL{l}", [128, 99], F32) for l in range(2)]
        self.colb = Buf("cols")
        self.scT = self.sb(P, "scT", [128, 16], BF16)
        self.scTb = Buf("scT")
        self.modT = [self.sb(P, f"modT{l}", [128, 144], F32) for l in range(2)]
        self.modb = Buf("modT")
        self.mcbl = [Buf("modc0"), Buf("modc1")]

        self.prologue()
        first = True
        np_ = len(self.phases)
        for i, (l, ph) in enumerate(self.phases):
            last = (i == np_ - 1)
            self.barrier()
            if ph == "f1":
                self.ffn_phase(l, 0, first, last)
            elif ph == "f2":
                self.ffn_phase(l, 2, first, last)
            elif ph == "mix":
                self.mix_a(l, first)
                self.barrier()
                self.mix_d(l)
                self.barrier()
                self.mix_c(l, last)
            first = False
        self.barrier()
        with nc.Block() as block:
            @block.sync
            def _(e):
                self.sp.replay(e)

            @block.tensor
            def _(e):
                self.pe.replay(e)

            @block.scalar
            def _(e):
                self.act.replay(e)

            @block.vector
            def _(e):
                self.dve.replay(e)

            @block.gpsimd
            def _(e):
                self.pool.replay(e)

    def casts_now(self, keys):
        todo = []
        for k in keys:
            for item in self.cast_rest:
                if item[0] == k:
                    todo.append(item)
        for item in todo:
            self.cast_rest.remove(item)
        if todo:
            self.issue_casts(todo)

    def mcol(self, l, seg, sub, kind, kc):
        idx = ((seg * 3 + sub) * 3 + kind) * 8 + kc
        return self.modc[l][:, idx:idx + 1]

    def prologue(self):
        nc, d, sc = self.nc, self.d, self.sc
        pe, act, dve, pool, sp = self.pe, self.act, self.dve, self.pool, self.sp
        self.wsem = {}
        cast_list = []
        for l in range(2):
            cast_list += [((l, "wi", 0), d["ffn1_wi"][l]), ((l, "wo", 0), d["ffn1_wo"][l]),
                          ((l, "win"), d["w_in"][l]), ((l, "wout"), d["w_out"][l]),
                          ((l, "wi", 2), d["ffn2_wi"][l]), ((l, "wo", 2), d["ffn2_wo"][l])]

        def issue_casts(items):
            for key, src in items:
                s = self.dsem("w" + "_".join(map(str, key)), barrier=False)
                dst = sc[key]
                if src.shape[-1] > 2048:
                    src = src.rearrange("r (a b) -> (r a) b", b=1408)
                    dst = dst.rearrange("r (a b) -> (r a) b", b=1408)
                self.dma(pool, dst, src, [], [self.scb[key]], s)
        with ExitStack() as st:
            sb = lambda n, s, t: self.sb(st, n, s, t)
            cs = self.dsem("pconst")
            cb = self.cbuf
            small_evs = []
            for dst, src in ((self.ident, d["ident"]), (self.identb, d["identb"]), (self.onesbd, d["onesbd"]),
                             (self.bdc, d["bdc"]), (self.bds, d["bds"])):
                small_evs.append(self.dma(sp, dst[:, :], src[:, :], [], [], cs))
            jm = sb("jm", [128, 128], F32)
            rb = sb("rb", [32, 8], F32)
            oh = sb("oh", [32, 1280], F32)
            s15 = sb("s15", [32, 128], F32)
            s31 = sb("s31", [32, 128], F32)
            selL = sb("selL", [128, 256], F32)
            selR = sb("selR", [128, 256], F32)
            mask = sb("mask", [128, 256], F32)
            for dst, src in ((jm, d["jmat"]), (rb, d["rel_bias"]), (oh, d["onehot"]), (s15, d["sel15"]),
                             (s31, d["sel31"]), (selL, d["selL"]), (selR, d["selR"]), (mask, d["mask"])):
                small_evs.append(self.dma(sp, dst[:, :], src[:, :], [], [], cs))
            pool.issue(lambda e: e.memset(self.onesb[:, :], 1.0), writes=[cb])
            pool.issue(lambda e: e.memset(self.epsc[:, :], EPS), writes=[cb])
            onesrow = sb("onesrow", [1, 128], F32)
            pool.issue(lambda e: e.memset(onesrow[:, :], 1.0), writes=[cb])
            stA = sb("stA", [16, 128], F32)
            stL = [sb(f"stL{l}", [99, 128], F32) for l in range(2)]
            stb = Buf("st")
            small_evs.append(self.dma(sp, stA[:, :], d["c2"][:, :], [], [], cs))
            for l in range(2):
                small_evs.append(self.dma(sp, stL[l][0:72, :], d["ada_b"][l], [], [], cs))
                small_evs.append(self.dma(sp, stL[l][72:80, :], d["norm_ffn1"][l], [], [], cs))
                small_evs.append(self.dma(sp, stL[l][80:88, :], d["norm_mix"][l], [], [], cs))
                small_evs.append(self.dma(sp, stL[l][88:96, :], d["norm_ffn2"][l], [], [], cs))
                small_evs.append(self.dma(sp, stL[l][96:97, 0:64], d["q_norm"][l], [], [], cs))
                small_evs.append(self.dma(sp, stL[l][96:97, 64:128], d["q_norm"][l], [], [], cs))
                small_evs.append(self.dma(sp, stL[l][97:98, 0:64], d["k_norm"][l], [], [], cs))
                small_evs.append(self.dma(sp, stL[l][97:98, 64:128], d["k_norm"][l], [], [], cs))
                small_evs.append(self.dma(sp, stL[l][98:99, :], d["subln"][l], [], [], cs))
            lq = [sb(f"lq{l}", [1, 256], F32) for l in range(2)]
            for l in range(2):
                small_evs.append(self.dma(sp, lq[l][:, :], d["lambda_qk"][l], [], [], cs))
            all_small = small_evs[-1]
            for e_ in (pe, act, dve, pool):
                e_.wait_event(all_small)
            colsA = sb("colsA", [128, 16], F32)
            colsL = self.colsL
            colb = self.colb
            b0 = self.nextbank()
            self.tr(self.ps[b0][:, 0:16], stA[:, :], self.ident[0:16, 0:16], [stb, cb], [self.psb[b0]])
            self.cp(dve, colsA[:, :], self.ps[b0][:, 0:16], [self.psb[b0]], [colb])
            for l in range(2):
                b0 = self.nextbank()
                self.tr(self.ps[b0][:, 0:99], stL[l][:, :], self.ident[0:99, 0:99], [stb, cb], [self.psb[b0]])
                self.cp(dve, colsL[l][:, :], self.ps[b0][:, 0:99], [self.psb[b0]], [colb])
            self.actf(self.scT[:, :], colsA[:, :], AF.Silu, [colb], [self.scTb])
            bk = self.nextbank()
            g0 = self.ada_gen(0, st, 1024, bk, nslots=2, cast_engs=[dve, act])
            next(g0)
            pool.wait_event(self.ada_evs[1])
            issue_casts(cast_list[0:2])
            for _ in g0:
                pass
            self.issue_casts = issue_casts
            self.cast_rest = cast_list[4:]
            self.cast_done = set()
            qkb = Buf("qkcol")
            self.qkb = qkb
            lam_t = sb("lam_t", [1, 64], F32)
            lam_s = sb("lam_s", [1, 4], F32)
            lamb = Buf("lam")
            for l in range(2):
                li = 0.8 - 0.6 * math.exp(-0.3 * l)
                self.cp(dve, self.qkcol[l][:, 0:2], colsL[l][:, 96:98], [colb], [qkb])
                self.ts(dve, self.qkcol[l][:, 2:3], colsL[l][:, 98:99], 1.0 - li, None, ALU.mult, None, [colb], [qkb])
                for j in range(2):
                    self.tt(dve, lam_t[:, :], lq[l][:, 128 * j:128 * j + 64], lq[l][:, 128 * j + 64:128 * j + 128],
                            ALU.mult, [stb], [lamb])
                    dve.issue(lambda e, j=j: e.reduce_sum(out=lam_s[:, j:j + 1], in_=lam_t[:, :],
                                                          axis=mybir.AxisListType.X), reads=[lamb], writes=[lamb])
                self.actf(lam_s[:, 2:4], lam_s[:, 0:2], AF.Exp, [lamb], [lamb])
                self.tt(dve, lam_s[:, 0:1], lam_s[:, 3:4], lam_s[:, 2:3], ALU.subtract, [lamb], [lamb])
                self.ts(dve, lam_s[:, 1:2], lam_s[:, 0:1], -li, None, ALU.add, None, [lamb], [lamb])
                bk = self.nextbank()
                self.mm(self.ps[bk][:, 0:1], onesrow[:, :], lam_s[:, 1:2], True, True, [cb, lamb], [self.psb[bk]], inc=True)
                self.cp(dve, self.qkcol[l][:, 3:4], self.ps[bk][:, 0:1], [self.psb[bk]], [qkb])
            gsb = sb("gsb", [8, 1280], F32)
            gb = Buf("gsb")
            for (c0, c1) in ((0, 512), (512, 1024), (1024, 1280)):
                bk = self.nextbank()
                self.mm(self.ps[bk][0:8, 0:c1 - c0], rb[:, :], oh[:, c0:c1], True, True, [cb], [self.psb[bk]], inc=True)
                self.cp(dve, gsb[:, c0:c1], self.ps[bk][0:8, 0:c1 - c0], [self.psb[bk]], [gb])
            gs = self.dsem("gs")
            self.dma(sp, sc["G"][:, :], gsb[:, :], [gb], [self.scb["G"]], gs)
            hank = [sb(f"hank{i}", [128, 1152], F32) for i in range(2)]
            hb = [Buf(f"hank{i}") for i in range(2)]
            hs = [self.dsem(f"hank{i}") for i in range(2)]
            sb_ = Buf("strips")
            self.stripb = sb_
            for hj in range(8):
                slot = hj % 2
                src = bass.AP(sc["G"].tensor, hj * 1280, [[1, 128], [1, 1152]])
                last_hank = self.dma(sp, hank[slot][:, :], src, [self.scb["G"]], [hb[slot]], hs[slot])
                for (c0, c1) in ((0, 512), (512, 1024), (1024, 1152)):
                    bk = self.nextbank()
                    self.mm(self.ps[bk][:, 0:c1 - c0], jm[:, :], hank[slot][:, c0:c1], True, True,
                            [cb, hb[slot]], [self.psb[bk]], inc=True)
                    self.cp(act if (c0 == 512) else dve, self.strips[:, hj, c0:c1], self.ps[bk][:, 0:c1 - c0],
                            [self.psb[bk]], [sb_])
            pool.wait_event(last_hank)
            issue_casts(cast_list[2:4])
            for hj in range(8):
                self.actf(self.strips[:, hj, :], self.strips[:, hj, :], AF.Exp, [sb_], [sb_])
            bcols = sb("bcols", [128, 16], F32)
            bcb = Buf("bcols")
            bk = self.nextbank()
            self.mm(self.ps[bk][:, 0:8], s15[:, :], rb[:, :], True, True, [cb], [self.psb[bk]], inc=False)
            self.mm(self.ps[bk][:, 8:16], s31[:, :], rb[:, :], False, True, [cb], [self.psb[bk]], inc=True,
                    skip_group_check=True)
            self.cp(dve, bcols[:, :], self.ps[bk][:, 0:16], [self.psb[bk]], [bcb])
            tmpt = sb("tmpt", [128, 256], F32)
            tb = Buf("tmpt")
            self.tableb = Buf("table")
            for hj in range(8):
                self.stt(dve, tmpt[:, :], selL[:, :], bcols[:, hj:hj + 1], mask[:, :], ALU.mult, ALU.add,
                         [cb, bcb], [tb])
                self.stt(dve, self.table[:, hj, :], selR[:, :], bcols[:, 8 + hj:9 + hj], tmpt[:, :], ALU.mult, ALU.add,
                         [cb, bcb, tb], [self.tableb])
            self.barrier()

    def ada_gen(self, l, st, gcols, bank, nslots=2, cast_engs=None):
        d = self.d
        pe, dve, sp, act = self.pe, self.dve, self.sp, self.act
        if cast_engs is None:
            cast_engs = [dve]
        ng = (9 * D) // gcols
        mpg = gcols // 128
        adaw = [self.sb(st, f"adaw{i}", [128, 8, gcols], F32) for i in range(nslots)]
        adab = [Buf(f"adaw{i}") for i in range(nslots)]
        adas = [self.dsem(f"adaw{i}") for i in range(nslots)]
        adah = [self.sb(st, f"adah{i}", [128, 8, gcols], BF16) for i in range(nslots)]
        adahb = [Buf(f"adah{i}") for i in range(nslots)]
        colsL, modT, modb, colb = self.colsL, self.modT, self.modb, self.colb
        scv = self.scT[:, :].rearrange("p (s k) -> p k s", k=8)
        self.ada_evs = []

        def load(gi):
            slot = gi % nslots
            self.last_ada_ev = self.dma(
                sp, adaw[slot][:, :, :],
                d["ada_w"][l][:, gi * gcols:(gi + 1) * gcols].rearrange("(kc p) n -> p kc n", p=128),
                [], [adab[slot]], adas[slot])
            self.ada_evs.append(self.last_ada_ev)
        for gi in range(min(nslots, ng)):
            load(gi)
        for gi in range(ng):
            slot = gi % nslots
            ne = len(cast_engs)
            for ci, eng in enumerate(cast_engs):
                k0, k1 = ci * 8 // ne, (ci + 1) * 8 // ne
                self.cp(eng, adah[slot][:, k0:k1, :], adaw[slot][:, k0:k1, :], [adab[slot]], [adahb[slot]])
            if gi + nslots < ng:
                load(gi + nslots)
            for mcl in range(mpg):
                for kc in range(KC):
                    first = (mcl == 0 and kc == 0)
                    lastg = (mcl == mpg - 1 and kc == KC - 1)
                    self.mm(self.ps[bank][:, 2 * mcl:2 * mcl + 2], adah[slot][:, kc, mcl * 128:(mcl + 1) * 128],
                            scv[:, kc, :], first, kc == KC - 1, [adahb[slot], self.scTb],
                            [self.psb[bank]] if (first or lastg) else [], inc=lastg, skip_group_check=True)
            pv = self.ps[bank][:, 0:2 * mpg].rearrange("p (m s) -> p m s", s=2)
            m0 = gi * mpg
            mv = modT[l][:, 2 * m0:2 * (m0 + mpg)].rearrange("p (m s) -> p m s", s=2)
            for seg in range(2):
                self.tt(dve, mv[:, :, seg], pv[:, :, seg], colsL[l][:, m0:m0 + mpg], ALU.add,
                        [self.psb[bank], colb], [modb])
            yield
        mcb = self.mcbl[l]
        mv = modT[l][:, :].rearrange("p (m s) -> p m s", s=2)
        for seg in range(2):
            for sub in range(3):
                base = ((seg * 3 + sub) * 3) * 8
                A = self.modc[l][:, base:base + 8]
                Bc = self.modc[l][:, base + 8:base + 16]
                G = self.modc[l][:, base + 16:base + 24]
                shift = mv[:, (3 * sub) * 8:(3 * sub) * 8 + 8, seg]
                scale = mv[:, (3 * sub + 1) * 8:(3 * sub + 1) * 8 + 8, seg]
                gate = mv[:, (3 * sub + 2) * 8:(3 * sub + 2) * 8 + 8, seg]
                gain = colsL[l][:, 72 + 8 * sub:80 + 8 * sub]
                self.stt(dve, A, scale, 1.0, gain, ALU.add, ALU.mult, [modb, colb], [mcb])
                self.cp(dve, Bc, shift, [modb], [mcb])
                self.ts(dve, G, gate, 1.0 if sub == 1 else 0.5, None, ALU.mult, None, [modb], [mcb])
        yield

    def load_xT(self, tt, xT, xb, xsem, first, xin=None, xinb=None, xins=None, store_scratch=False, st_sem=None):
        sp, pe, dve, act = self.sp, self.pe, self.dve, self.act
        if not first:
            src = self.sc["xT"][:, :, tt * TT:(tt + 1) * TT].rearrange("kc p t -> p kc t")
            self.dma(sp, xT[:, :, :], src, [self.xTb[tt]], [xb], xsem)
            return
        src = self.d["x"][tt * TT:(tt + 1) * TT, :].rearrange("(s p) c -> p s c", p=128)
        self.dma(sp, xin[:, :, :], src, [], [xinb], xins)
        for kc in range(KC):
            bk = self.nextbank()
            for s in range(4):
                self.tr(self.ps[bk][:, s * 128:(s + 1) * 128], xin[:, s, kc * 128:(kc + 1) * 128], self.ident[:, :],
                        [xinb], [self.psb[bk]] if s in (0, 3) else [], inc=(s == 3))
            self.cp(act if kc % 2 else dve, xT[:, kc, :], self.ps[bk][:, :], [self.psb[bk]], [xb])
        if store_scratch:
            dst = self.sc["xT"][:, :, tt * TT:(tt + 1) * TT].rearrange("kc p t -> p kc t")
            self.dma(self.pool, dst, xT[:, :, :], [xb], [self.xTb[tt]], st_sem)

    def store_xT(self, tt, xT, xb, xsem, last, yout=None, youtb=None, youts=None):
        pool, pe, dve, act = self.pool, self.pe, self.dve, self.act
        if not last:
            dst = self.sc["xT"][:, :, tt * TT:(tt + 1) * TT].rearrange("kc p t -> p kc t")
            self.dma(pool, dst, xT[:, :, :], [xb], [self.xTb[tt]], xsem)
            return
        for s in range(4):
            for half in range(2):
                bk = self.nextbank()
                for k4 in range(4):
                    kc = half * 4 + k4
                    self.tr(self.ps[bk][:, k4 * 128:(k4 + 1) * 128], xT[:, kc, s * 128:(s + 1) * 128], self.ident[:, :],
                            [xb], [self.psb[bk]] if k4 in (0, 3) else [], inc=(k4 == 3))
                self.cp(act if half else dve, yout[:, s, half * 512:(half + 1) * 512], self.ps[bk][:, :],
                        [self.psb[bk]], [youtb])
        dst = self.y[tt * TT:(tt + 1) * TT, :].rearrange("(s p) c -> p s c", p=128)
        self.dma(pool, dst, yout[:, :, :], [youtb], [], youts)

    def norm_p1(self, xT, xb, sq, sqb):
        bk = self.nextbank()
        for kc in range(KC):
            i = kc % 2
            self.actf(sq[i][:, :], xT[:, kc, :], AF.Square, [xb], [sqb[i]])
            self.mm(self.ps[bk][:, :], self.onesb[:, :], sq[i][:, :], kc == 0, kc == KC - 1,
                    [sqb[i]], [self.psb[bk]] if kc in (0, KC - 1) else [], inc=True)
        return bk

    def norm_p2(self, bk, l, sub, seg, xT, xb, hT, hb, rstd, rsb, tmp, tmpb):
        dve = self.dve
        self.actf(rstd[:, :], self.ps[bk][:, :], AF.Ln, [self.psb[bk], self.cbuf], [rsb], bias=self.epsc[:, :], scale=1.0 / D)
        self.actf(rstd[:, :], rstd[:, :], AF.Exp, [rsb], [rsb], scale=-0.5)
        for kc in range(KC):
            i = kc % 2
            self.stt(dve, tmp[i][:, :], xT[:, kc, :], self.mcol(l, seg, sub, 0, kc), rstd[:, :], ALU.mult, ALU.mult,
                     [xb, rsb, self.mcbl[l]], [tmpb[i]])
            self.actf(hT[:, kc, :], tmp[i][:, :], AF.Identity, [tmpb[i], self.mcbl[l]], [hb],
                      bias=self.mcol(l, seg, sub, 1, kc), scale=1.0)

    def norm_mod(self, l, sub, seg, xT, xb, hT, hb, sq, sqb, rstd, rsb, tmp, tmpb):
        bk = self.norm_p1(xT, xb, sq, sqb)
        self.norm_p2(bk, l, sub, seg, xT, xb, hT, hb, rstd, rsb, tmp, tmpb)

    def ffn_phase(self, l, sub, first, last):
        sc = self.sc
        self.casts_now([(l, "wi", sub), (l, "wo", sub)])
        if l == 0 and sub == 2:
            self.casts_now([(1, "wi", 0), (1, "wo", 0), (1, "win"), (1, "wout"), (1, "wi", 2), (1, "wo", 2)])
        if l == 1:
            self.casts_now([k for k, _ in list(self.cast_rest)])
        pe, act, dve, pool, sp = self.pe, self.act, self.dve, self.pool, self.sp
        wi_d = sc[(l, "wi", sub)]
        wo_d = sc[(l, "wo", sub)]
        wib_, wob_ = self.scb[(l, "wi", sub)], self.scb[(l, "wo", sub)]
        wi_v = wi_d.rearrange("(kc p) n -> p kc n", p=128)
        wo_v = wo_d.rearrange("(fc p) m -> p fc m", p=128)
        with ExitStack() as st:
            sb = lambda n, s, t: self.sb(st, n, s, t)
            xT = [sb(f"xT{i}", [128, KC, TT], F32) for i in range(2)]
            xb = [Buf(f"xT{i}") for i in range(2)]
            xs = [self.dsem(f"xT{i}") for i in range(2)]
            xst = [self.dsem(f"xTst{i}") for i in range(2)]
            hTs = [sb(f"hT{i}", [128, KC, TT], BF16) for i in range(2)]; hbs = [Buf("hT0"), Buf("hT1")]
            sq = [sb(f"sq{i}", [128, TT], BF16) for i in range(2)]; sqb = [Buf(), Buf()]
            rstd = sb("rstd", [128, TT], F32); rsb = Buf("rstd")
            tmp = [sb(f"tmp{i}", [128, TT], F32) for i in range(2)]; tmpb = [Buf(), Buf()]
            actT = sb("actT", [128, FC, TT], BF16); ab = [Buf(f"actT{f}") for f in range(FC)]
            sg = [sb(f"sg{i}", [128, TT], F32) for i in range(2)]; sgb = [Buf(), Buf()]
            NWI, NWO = 3, (2 if (first or last) else 3)
            wi = [sb(f"wi{i}", [128, 2, KC, 256], BF16) for i in range(NWI)]
            wib = [Buf(f"wi{i}") for i in range(NWI)]
            wis = [self.dsem(f"wi{i}") for i in range(NWI)]
            wo = [sb(f"wo{i}", [128, FC, 256], BF16) for i in range(NWO)]
            wob = [Buf(f"wo{i}") for i in range(NWO)]
            wos = [self.dsem(f"wo{i}") for i in range(NWO)]
            xin = xinb = xins = yout = youtb = youts = None
            if first:
                xin = sb("xin", [128, 4, D], F32); xinb = Buf("xin"); xins = self.dsem("xin")
            if last:
                yout = sb("yout", [128, 4, D], F32); youtb = Buf("yout"); youts = self.dsem("yout")
            wic = 0
            woc = 0
            self.load_xT(0, xT[0], xb[0], xs[0], first, xin, xinb, xins)
            self.norm_mod(l, sub, 0, xT[0], xb[0], hTs[0], hbs[0], sq, sqb, rstd, rsb, tmp, tmpb)
            for tt in range(NTT):
                cur = tt % 2
                seg = tt // 4
                X, XB = xT[cur], xb[cur]
                hT, hb = hTs[cur], hbs[cur]
                for j in range(11):
                    slot = wic % NWI
                    wic += 1
                    for gu in range(2):
                        c0 = gu * DFF + j * 256
                        self.dma(sp, wi[slot][:, gu, :, :], wi_v[:, :, c0:c0 + 256], [wib_], [wib[slot]], wis[slot])
                    if j == 3 and tt + 1 < NTT:
                        self.load_xT(tt + 1, xT[1 - cur], xb[1 - cur], xs[1 - cur], first, xin, xinb, xins)
                    if j == 7 and tt + 1 < NTT:
                        nbk = self.norm_p1(xT[1 - cur], xb[1 - cur], sq, sqb)
                    if j == 8 and tt + 1 < NTT:
                        self.norm_p2(nbk, l, sub, (tt + 1) // 4, xT[1 - cur], xb[1 - cur], hTs[1 - cur], hbs[1 - cur],
                                     rstd, rsb, tmp, tmpb)
                    for fcl in range(2):
                        fc = 2 * j + fcl
                        bg = self.nextbank()
                        bu = self.nextbank()
                        for gu, bk in ((0, bg), (1, bu)):
                            for kc in range(KC):
                                self.mm(self.ps[bk][:, :], wi[slot][:, gu, kc, fcl * 128:(fcl + 1) * 128], hT[:, kc, :],
                                        kc == 0, kc == KC - 1, [wib[slot], hb],
                                        [self.psb[bk]] if kc in (0, KC - 1) else [], inc=(kc == KC - 1))
                        i = fc % 2
                        self.actf(sg[i][:, :], self.ps[bg][:, :], AF.Silu, [self.psb[bg]], [sgb[i]])
                        self.tt(dve, actT[:, fc, :], sg[i][:, :], self.ps[bu][:, :], ALU.mult,
                                [sgb[i], self.psb[bu]], [ab[fc]])
                for mb in range(4):
                    slot = woc % NWO
                    woc += 1
                    self.dma(sp, wo[slot][:, :, :], wo_v[:, :, mb * 256:(mb + 1) * 256], [wob_], [wob[slot]], wos[slot])
                    for ml in range(2):
                        mc = 2 * mb + ml
                        bk = self.nextbank()
                        for fc in range(FC):
                            self.mm(self.ps[bk][:, :], wo[slot][:, fc, ml * 128:(ml + 1) * 128], actT[:, fc, :],
                                    fc == 0, fc == FC - 1, [wob[slot], ab[fc]],
                                    [self.psb[bk]] if fc in (0, FC - 1) else [], inc=(fc == FC - 1))
                        self.stt(dve, X[:, mc, :], self.ps[bk][:, :], self.mcol(l, seg, sub, 2, mc), X[:, mc, :],
                                 ALU.mult, ALU.add, [self.psb[bk], self.mcbl[l]], [XB])
                self.store_xT(tt, X, XB, xst[cur], last, yout, youtb, youts)

    def mix_a(self, l, first):
        sc, d = self.sc, self.d
        pe, act, dve, pool, sp = self.pe, self.act, self.dve, self.pool, self.sp
        with ExitStack() as st:
            sb = lambda n, s, t: self.sb(st, n, s, t)
            xT = [sb(f"xT{i}", [128, KC, TT], F32) for i in range(2)]
            xb = [Buf(f"xT{i}") for i in range(2)]
            xs = [self.dsem(f"xT{i}") for i in range(2)]
            xss = [self.dsem(f"xTst{i}") for i in range(2)]
            hTs = [sb(f"hT{i}", [128, KC, TT], BF16) for i in range(2)]; hbs = [Buf("hT0"), Buf("hT1")]
            sq = [sb(f"sq{i}", [128, TT], BF16) for i in range(2)]; sqb = [Buf(), Buf()]
            rstd = sb("rstd", [128, TT], F32); rsb = Buf("rstd")
            tmp = [sb(f"tmp{i}", [128, TT], F32) for i in range(2)]; tmpb = [Buf(), Buf()]
            win = sb("win", [128, KC, 2048], BF16); winb = Buf("win"); wins = self.dsem("win")
            qsq = [sb(f"qsq{i}", [128, TT], BF16) for i in range(2)]; qsqb = [Buf(), Buf()]
            qrs = [sb(f"qrs{i}", [128, TT], F32) for i in range(2)]; qrsb = [Buf(), Buf()]
            NQ = 3
            qo = [sb(f"qo{i}", [128, TT], BF16) for i in range(NQ)]; qob = [Buf() for _ in range(NQ)]
            qos = [self.dsem(f"qo{i}") for i in range(NQ)]
            vst = [sb(f"vst{i}", [128, NH, 129], BF16) for i in range(2)]; vstb = [Buf(), Buf()]
            vss = [self.dsem(f"vst{i}") for i in range(2)]
            fT = sb("fT", [128, 4, TT], BF16); fTb = Buf("fT")
            xcs = [sb(f"xcs{i}", [128, 2, 512], BF16) for i in range(2)]; xcsb = [Buf(), Buf()]
            xcss = [self.dsem(f"xcs{i}") for i in range(2)]
            xcss2 = [self.dsem(f"xcsb{i}") for i in range(2)]
            xin = xinb = xins = None
            if first:
                xin = sb("xin", [128, 4, D], F32); xinb = Buf("xin"); xins = self.dsem("xin")
            self.casts_now([(l, "win"), (l, "wout")])
            self.load_xT(0, xT[0], xb[0], xs[0], first, xin, xinb, xins, store_scratch=first, st_sem=xss[0])
            wv = sc[(l, "win")].rearrange("(kc p) n -> p kc n", p=128)
            for h2 in range(2):
                self.dma(sp, win[:, :, h2 * 1024:(h2 + 1) * 1024], wv[:, :, h2 * 1024:(h2 + 1) * 1024],
                         [self.scb[(l, "win")]], [winb], wins)
            for i in range(2):
                pool.issue(lambda e, i=i: e.memset(vst[i][:, :, 128:129], 1.0), writes=[vstb[i]])
            qc = 0
            vc = 0
            xc = 0
            fTs = [fT, sb("fT1", [128, 4, TT], BF16)]
            fTbs = [fTb, Buf("fT1")]

            def xcs_section(t_, subs=(0, 1, 2, 3)):
                nonlocal xc
                fT_, fTb_ = fTs[t_ % 2], fTbs[t_ % 2]
                for s in subs:
                    xi = xc % 2
                    xc += 1
                    for ci, bdm in enumerate((self.bdc, self.bds)):
                        bx = self.nextbank()
                        for fi in range(4):
                            self.mm(self.ps[bx][:, fi * 128:(fi + 1) * 128], fT_[:, fi, s * 128:(s + 1) * 128], bdm[:, :],
                                    True, True, [fTb_, self.cbuf], [self.psb[bx]] if fi in (0, 3) else [], inc=(fi == 3),
                                    skip_group_check=True)
                        self.cp(dve, xcs[xi][:, ci, :], self.ps[bx][:, :], [self.psb[bx]], [xcsb[xi]])
                    r0 = t_ * TT + s * 128
                    self.dma(pool, sc["XC"][r0:r0 + 128, :], xcs[xi][:, 0, :], [xcsb[xi]], [self.scb["XC"]], xcss[xi])
                    self.dma(pool, sc["XS"][r0:r0 + 128, :], xcs[xi][:, 1, :], [xcsb[xi]], [self.scb["XS"]], xcss2[xi])

            self.norm_mod(l, 1, 0, xT[0], xb[0], hTs[0], hbs[0], sq, sqb, rstd, rsb, tmp, tmpb)
            for tt in range(NTT):
                cur = tt % 2
                seg = tt // 4
                X, XB = xT[cur], xb[cur]
                hT, hb = hTs[cur], hbs[cur]
                if tt + 1 < NTT:
                    self.load_xT(tt + 1, xT[1 - cur], xb[1 - cur], xs[1 - cur], first, xin, xinb, xins,
                                 store_scratch=first, st_sem=xss[1 - cur])

                def qk_finish(g, bq, i):
                    nonlocal qc
                    isk = g >= 4
                    h = g % 4
                    bs = self.nextbank()
                    self.mm(self.ps[bs][:, :], self.onesbd[:, :], qsq[i][:, :], True, True, [qsqb[i]], [self.psb[bs]], inc=True)
                    self.actf(qrs[i][:, :], self.ps[bs][:, :], AF.Ln, [self.psb[bs], self.cbuf], [qrsb[i]],
                              bias=self.epsc[:, :], scale=1.0 / HD)
                    self.actf(qrs[i][:, :], qrs[i][:, :], AF.Exp, [qrsb[i]], [qrsb[i]], scale=-0.5)
                    qi = qc % NQ
                    qc += 1
                    self.stt(dve, qo[qi][:, :], self.ps[bq][:, :], self.qkcol[l][:, (1 if isk else 0):(2 if isk else 1)],
                             qrs[i][:, :], ALU.mult, ALU.mult, [self.psb[bq], qrsb[i], self.qkb], [qob[qi]])
                    dst = sc["KT" if isk else "QT"][h, :, tt * TT:(tt + 1) * TT]
                    self.dma(pool, dst, qo[qi][:, :], [qob[qi]], [self.scb["KT" if isk else "QT"]], qos[qi])
                pend = None
                for g in range(8):
                    bq = self.nextbank()
                    for kc in range(KC):
                        self.mm(self.ps[bq][:, :], win[:, kc, g * 128:(g + 1) * 128], hT[:, kc, :], kc == 0, kc == KC - 1,
                                [winb, hb], [self.psb[bq]] if kc in (0, KC - 1) else [], inc=(kc == KC - 1))
                    i = g % 2
                    self.actf(qsq[i][:, :], self.ps[bq][:, :], AF.Square, [self.psb[bq]], [qsqb[i]])
                    if pend is not None:
                        qk_finish(*pend)
                    pend = (g, bq, i)
                    if tt > 0 and g % 2 == 1:
                        xcs_section(tt - 1, subs=(g // 2,))
                nbk = None
                if tt + 1 < NTT:
                    nbk = self.norm_p1(xT[1 - cur], xb[1 - cur], sq, sqb)
                qk_finish(*pend)
                if nbk is not None:
                    self.norm_p2(nbk, l, 1, (tt + 1) // 4, xT[1 - cur], xb[1 - cur], hTs[1 - cur], hbs[1 - cur],
                                 rstd, rsb, tmp, tmpb)
                for s in range(4):
                    bv = self.nextbank()
                    for kc in range(KC):
                        self.mm(self.ps[bv][:, :], hT[:, kc, s * 128:(s + 1) * 128], win[:, kc, 1024:1536], kc == 0, kc == KC - 1,
                                [winb, hb], [self.psb[bv]] if kc in (0, KC - 1) else [], inc=(kc == KC - 1))
                    vi = vc % 2
                    vc += 1
                    self.cp(dve, vst[vi][:, :, 0:128], self.ps[bv][:, :].rearrange("p (h d) -> p h d", h=NH),
                            [self.psb[bv]], [vstb[vi]])
                    r0 = tt * TT + s * 128
                    self.dma(pool, sc["V"][r0:r0 + 128, :], vst[vi][:, :, :].rearrange("p h d -> p (h d)"),
                             [vstb[vi]], [self.scb["V"]], vss[vi])
                fT, fTb = fTs[cur], fTbs[cur]
                for fi in range(4):
                    bf = self.nextbank()
                    for kc in range(KC):
                        self.mm(self.ps[bf][:, :], win[:, kc, 1536 + fi * 128:1536 + (fi + 1) * 128], hT[:, kc, :],
                                kc == 0, kc == KC - 1, [winb, hb], [self.psb[bf]] if kc in (0, KC - 1) else [],
                                inc=(kc == KC - 1))
                    self.cp(dve, fT[:, fi, :], self.ps[bf][:, :], [self.psb[bf]], [fTb])
            xcs_section(NTT - 1)

    def mix_d(self, l):
        sc, d = self.sc, self.d
        pe, act, dve, pool, sp = self.pe, self.act, self.dve, self.pool, self.sp
        with ExitStack() as st:
            sb = lambda n, s, t: self.sb(st, n, s, t)
            XC = sb("XC", [128, NKC, 512], BF16); XCb = Buf("XC")
            XS = sb("XS", [128, NKC, 512], BF16); XSb = Buf("XS")
            xls = self.dsem("xcl"); xls2 = self.dsem("xsl")
            NR = 3
            ring = [sb(f"dr{i}", [128, 4, 2, 512], BF16) for i in range(NR)]
            rb = [Buf(f"dr{i}") for i in range(NR)]
            rs = [self.dsem(f"dr{i}") for i in range(NR)]
            NO = 4
            fo = [sb(f"fo{i}", [128, 512], BF16) for i in range(NO)]
            fob = [Buf() for _ in range(NO)]
            fos = [self.dsem(f"fo{i}") for i in range(NO)]
            XCbs = [XCb, Buf("XCh1")]
            XSbs = [XSb, Buf("XSh1")]
            xlsh = [xls, self.dsem("xcl1")]
            xls2h = [xls2, self.dsem("xsl1")]

            def load_half(half):
                r = slice(half * 16, (half + 1) * 16)
                self.dma(sp, XC[:, r, :], sc["XC"][half * 2048:(half + 1) * 2048, :].rearrange("(k p) f -> p k f", p=128),
                         [self.scb["XC"]], [XCbs[half]], xlsh[half])
                self.dma(sp, XS[:, r, :], sc["XS"][half * 2048:(half + 1) * 2048, :].rearrange("(k p) f -> p k f", p=128),
                         [self.scb["XS"]], [XSbs[half]], xls2h[half])
            load_half(0)
            rc = 0
            oc = 0
            for j in range(8):
                banks = [self.nextbank() for _ in range(4)]
                for ig in range(8):
                    slot = rc % NR
                    rc += 1
                    self.dma(sp, ring[slot][:, :, :, :].rearrange("p a b c -> p (a b c)"), d["dftm"][j * 8 + ig, :, :],
                             [], [rb[slot]], rs[slot])
                    if j == 0 and ig == 1:
                        load_half(1)
                    for i4 in range(4):
                        k = ig * 4 + i4
                        for ci, (XM, XMb) in enumerate(((XC, XCbs[k // 16]), (XS, XSbs[k // 16]))):
                            for fi in range(4):
                                st_ = (k == 0 and ci == 0)
                                sp_ = (k == NKC - 1 and ci == 1)
                                self.mm(self.ps[banks[fi]][:, :], XM[:, k, fi * 128:(fi + 1) * 128], ring[slot][:, i4, ci, :],
                                        st_, sp_, [XMb, rb[slot]], [self.psb[banks[fi]]] if (st_ or sp_) else [],
                                        inc=(sp_ or (i4 == 3 and ci == 1 and fi == 3)))
                for fi in range(4):
                    oi = oc % NO
                    oc += 1
                    self.cp(act if fi % 2 else dve, fo[oi][:, :], self.ps[banks[fi]][:, :], [self.psb[banks[fi]]], [fob[oi]])
                    self.dma(pool, sc["FOT"][fi, :, j * 512:(j + 1) * 512], fo[oi][:, :], [fob[oi]], [self.scb["FOT"]], fos[oi])

    def mix_c(self, l, last):
        sc, d = self.sc, self.d
        pe, act, dve, pool, sp = self.pe, self.act, self.dve, self.pool, self.sp
        li = 0.8 - 0.6 * math.exp(-0.3 * l)
        with ExitStack() as st:
            sb = lambda n, s, t: self.sb(st, n, s, t)
            KT = sb("KT", [128, NH, NT], BF16); KTbs = [Buf(f"KT{h}") for h in range(NH)]
            ktss = [self.dsem(f"KT{h}") for h in range(NH)]
            V = sb("V", [128, NKC, NH * 129], BF16); Vb = Buf("V"); vs = self.dsem("V")
            wout = sb("wout", [128, KC, D], BF16); woutb = Buf("wout"); wouts = self.dsem("wout")
            QT = [sb(f"QT{i}", [128, NH, TT], BF16) for i in range(2)]; QTb = [Buf(), Buf()]
            qts = [self.dsem(f"QT{i}") for i in range(2)]
            foT = [sb(f"foT{i}", [128, 4, TT], BF16) for i in range(2)]; foTb = [Buf(), Buf()]
            fots = [self.dsem(f"foT{i}") for i in range(2)]
            xT = sb("xT", [128, KC, TT], F32); xb = Buf("xT"); xs = self.dsem("xT"); xst = self.dsem("xTst0")
            attnTs = [sb(f"attnT{i}", [128, NH, TT], BF16) for i in range(2)]; atbs = [Buf("attnT0"), Buf("attnT1")]
            NP = 3
            Pm = [[sb(f"P{j}_{i}", [128, TT], BF16) for i in range(NP)] for j in range(2)]
            Pb = [[Buf() for _ in range(NP)] for _ in range(2)]
            mt = [sb(f"mt{i}", [128, TT], F32) for i in range(2)]; mtb = [Buf(), Buf()]
            t2 = [sb(f"t2{i}", [128, 128], F32) for i in range(2)]; t2b = [Buf(), Buf()]
            yout = youtb = youts = None
            if last:
                yout = sb("yout", [128, 4, D], F32); youtb = Buf("yout"); youts = self.dsem("yout")
            cast_plan = {}
            if l == 0:
                cast_plan = {40: (0, "wi", 2), 200: (0, "wo", 2), 500: (1, "wi", 0), 650: (1, "wo", 0),
                             800: (1, "win"), 850: (1, "wout")}
            def load_resident():
                self.dma(sp, KT[:, 0, :], sc["KT"][0, :, :], [self.scb["KT"]], [KTbs[0]], ktss[0])
                for half in range(2):
                    self.dma(sp, V[:, half * 16:(half + 1) * 16, :],
                             sc["V"][half * 2048:(half + 1) * 2048, :].rearrange("(k p) f -> p k f", p=128),
                             [self.scb["V"]], [Vb], vs)
                for h in range(1, NH):
                    self.dma(sp, KT[:, h, :], sc["KT"][h, :, :], [self.scb["KT"]], [KTbs[h]], ktss[h])
                self.dma(sp, wout[:, :, :], sc[(l, "wout")].rearrange("(kc p) n -> p kc n", p=128),
                         [self.scb[(l, "wout")]], [woutb], wouts)

            def load_q(qt, i):
                self.dma(sp, QT[i][:, :, :], sc["QT"][:, :, qt * TT:(qt + 1) * TT].rearrange("h p t -> p h t"),
                         [self.scb["QT"]], [QTb[i]], qts[i])

            def load_fo(qt, i):
                self.dma(sp, foT[i][:, :, :], sc["FOT"][:, :, qt * TT:(qt + 1) * TT].rearrange("f p t -> p f t"),
                         [self.scb["FOT"]], [foTb[i]], fots[i])
            load_q(0, 0)
            load_resident()
            load_fo(0, 0)
            load_fo(1, 1)
            SB = [[0, 1], [2, 3]]
            OB = [4, 5, 6]
            MISC = 7

            def oacc(j, s):
                idx = j * 4 + s
                return OB[idx // 3], (idx % 3) * 129
            ocp = sb("ocp", [128, 8 * 129], F32); ocb = Buf("ocp")
            its = [(qt, h, kc) for qt in range(NTT) for h in range(NH) for kc in range(NKC)]
            NIT = len(its)

            def emit_qk(idx):
                qt, h, kc = its[idx]
                sl = idx % 2
                cur = qt % 2
                for j in range(2):
                    bk = SB[sl][j]
                    self.mm(self.ps[bk][:, :], KT[64 * j:64 * (j + 1), h, kc * 128:(kc + 1) * 128],
                            QT[cur][64 * j:64 * (j + 1), h, :], True, True, [KTbs[h], QTb[cur]], [self.psb[bk]], inc=True)

            def emit_exp(idx):
                qt, h, kc = its[idx]
                sl = idx % 2
                pi = idx % NP
                dd = kc - 4 * qt
                mixed = (-1 <= dd <= 4)
                for j in range(2):
                    bk = SB[sl][j]
                    hj = h * 2 + j
                    col = self.table[:, hj, kc * NTT + qt:kc * NTT + qt + 1]
                    if mixed:
                        c0 = (4 - dd) * 128
                        self.actf(mt[j][:, :], self.ps[bk][:, :], AF.Exp, [self.psb[bk], self.tableb], [mtb[j]],
                                  bias=col, scale=0.125)
                        self.tt(dve, Pm[j][pi][:, :], mt[j][:, :], self.strips[:, hj, c0:c0 + TT], ALU.mult,
                                [mtb[j], self.stripb], [Pb[j][pi]])
                    else:
                        self.actf(Pm[j][pi][:, :], self.ps[bk][:, :], AF.Exp, [self.psb[bk], self.tableb],
                                  [Pb[j][pi]], bias=col, scale=0.125)

            def emit_av(idx):
                qt, h, kc = its[idx]
                pi = idx % NP
                for j in range(2):
                    for s in range(4):
                        ob, oc0 = oacc(j, s)
                        i8 = j * 4 + s
                        first_in_bank = (kc == 0 and i8 % 3 == 0)
                        lastk = (kc == NKC - 1)
                        self.mm(self.ps[ob][:, oc0:oc0 + 129], Pm[j][pi][:, s * 128:(s + 1) * 128],
                                V[:, kc, h * 129:(h + 1) * 129], first_in_bank, lastk,
                                [Pb[j][pi], Vb], [self.psb[ob]] if (first_in_bank or lastk) else [],
                                inc=(lastk or (j == 1 and s == 3)), skip_group_check=True)

            rr8 = sb("rr8", [128, 8], F32); rr8b = Buf("rr8")
            oo4 = [sb("oo4_0", [128, 4, 128], F32)] * 2; oo4b = [Buf()] * 2
            jk4 = sb("jk4", [128, 4, 128], F32); jk4b = Buf("jk4")
            ss4 = [sb(f"ss4_{i}", [128, 4], F32) for i in range(2)]; ss4b = [Buf(), Buf()]
            on4 = [sb("on4_0", [128, 4, 128], BF16)] * 2; on4b = [Buf()] * 2

            def emit_epilogue(qt, h, ec0):
                e = h % 2
                ocv = ocp[:, :].rearrange("p (a c) -> p a c", c=129)
                self.cp(dve, ocp[:, 0:387], self.ps[OB[0]][:, 0:387], [self.psb[OB[0]]], [ocb])
                self.cp(dve, ocp[:, 387:774], self.ps[OB[1]][:, 0:387], [self.psb[OB[1]]], [ocb])
                self.cp(dve, ocp[:, 774:1032], self.ps[OB[2]][:, 0:258], [self.psb[OB[2]]], [ocb])
                dve.issue(lambda e_: e_.reciprocal(out=rr8[:, :], in_=ocv[:, :, 128]), reads=[ocb], writes=[rr8b])
                for s in range(4):
                    i2 = s % 2
                    self.ts(dve, t2[i2][:, :], ocv[:, 4 + s, 0:128], rr8[:, 4 + s:5 + s], self.qkcol[l][:, 3:4],
                            ALU.mult, ALU.mult, [ocb, rr8b, self.qkb], [t2b[i2]])
                    self.stt(dve, oo4[e][:, s, :], ocv[:, s, 0:128], rr8[:, s:s + 1], t2[i2][:, :],
                             ALU.mult, ALU.add, [ocb, rr8b, t2b[i2]], [oo4b[e]])
                self.tt(dve, jk4[:, :, :], oo4[e][:, :, :], oo4[e][:, :, :], ALU.mult, [oo4b[e]], [jk4b])
                dve.issue(lambda e_, e=e: e_.reduce_sum(out=ss4[e][:, :], in_=jk4[:, :, :], axis=mybir.AxisListType.X),
                          reads=[jk4b], writes=[ss4b[e]])
                self.ts(dve, ss4[e][:, :], ss4[e][:, :], 1.0 / 128, EPS, ALU.mult, ALU.add, [ss4b[e]], [ss4b[e]])

                def part_a2(e=e):
                    self.actf(ss4[e][:, :], ss4[e][:, :], AF.Ln, [ss4b[e]], [ss4b[e]])
                    self.actf(ss4[e][:, :], ss4[e][:, :], AF.Exp, [ss4b[e]], [ss4b[e]], scale=-0.5)
                    for s in range(4):
                        self.ts(dve, on4[e][:, s, :], oo4[e][:, s, :], ss4[e][:, s:s + 1], None, ALU.mult, None,
                                [oo4b[e], ss4b[e]], [on4b[e]])

                def part_b(e=e, h=h, qt=qt):
                    pv = self.ps[MISC][:, :].bitcast(BF16)
                    for s in range(4):
                        self.tr(pv[:, s * 128:(s + 1) * 128], on4[e][:, s, :], self.identb[:, :], [on4b[e], self.cbuf],
                                [self.psb[MISC]] if s in (0, 3) else [], inc=(s == 3))
                    self.ts(dve, attnTs[qt % 2][:, h, :], pv[:, 0:512], self.qkcol[l][:, 2:3], None, ALU.mult, None,
                            [self.psb[MISC], self.qkb], [atbs[qt % 2]])
                return part_a2, part_b

            ec = 0
            adag = None
            if l == 0 and any(pl == 1 for pl, _ in self.phases):
                adag = self.ada_gen(1, st, 128, MISC, nslots=2)
            from collections import deque
            dq = deque()

            def wout_group(qt, mc):
                cur = qt % 2
                seg = qt // 4

                def f():
                    if mc == 0:
                        self.load_xT(qt, xT, xb, xs, False)
                    bk = MISC
                    for jc in range(KC):
                        if jc < 4:
                            rhs, rbuf = attnTs[cur][:, jc, :], atbs[cur]
                        else:
                            rhs, rbuf = foT[cur][:, jc - 4, :], foTb[cur]
                        self.mm(self.ps[bk][:, :], wout[:, jc, mc * 128:(mc + 1) * 128], rhs, jc == 0, jc == KC - 1,
                                [woutb, rbuf], [self.psb[bk]] if jc in (0, KC - 1) else [], inc=(jc == KC - 1))
                    self.stt(dve, xT[:, mc, :], self.ps[bk][:, :], self.mcol(l, seg, 1, 2, mc), xT[:, mc, :],
                             ALU.mult, ALU.add, [self.psb[bk], self.mcbl[l]], [xb])
                    if mc == KC - 1:
                        self.store_xT(qt, xT, xb, xst, last, yout, youtb, youts)
                        if qt + 2 < NTT:
                            load_fo(qt + 2, cur)
                return f

            emit_qk(0)
            emit_qk(1)
            for idx in range(NIT):
                qt, h, kc = its[idx]
                cur = qt % 2
                if h == 0 and kc == 0 and qt + 1 < NTT:
                    load_q(qt + 1, 1 - cur)
                if idx in cast_plan:
                    self.casts_now([cast_plan[idx]])
                emit_exp(idx)
                if idx + 2 < NIT:
                    emit_qk(idx + 2)
                emit_av(idx)
                if adag is not None and idx % 6 == 3:
                    if next(adag, "done") == "done":
                        adag = None
                elif dq and dq[0][0] <= idx:
                    dq.popleft()[1]()
                if kc == NKC - 1:
                    pa2, pb = emit_epilogue(qt, h, ec)
                    dq.append((idx + 7, pa2))
                    dq.append((idx + 10, pb))
                    if h == NH - 1:
                        for mc in range(KC):
                            dq.append((idx + 12 + 5 * mc, wout_group(qt, mc)))
            while dq:
                dq.popleft()[1]()
            if adag is not None:
                for _ in adag:
                    pass


FULL_PHASES = [(0, "f1"), (0, "mix"), (0, "f2"), (1, "f1"), (1, "mix"), (1, "f2")]
_NC_CACHE = {}


def _get_nc(phases):
    key = tuple(phases)
    if key not in _NC_CACHE:
        _NC_CACHE[key] = Builder(list(phases)).build()
    return _NC_CACHE[key]


def _run(inputs, phases):
    c = _host_consts()
    f32 = lambda a: np.ascontiguousarray(np.asarray(a, dtype=np.float32))
    xs = f32(inputs["x_sample"])
    xp = f32(inputs["x_prompt"])
    cs_ = f32(inputs["c_sample"])
    cp_ = f32(inputs["c_prompt"])
    shared = {
        "ada_w": f32(inputs["ada_w"]),
        "ada_b": f32(inputs["ada_b"]).reshape(2, 72, 128),
        "norm_ffn1": f32(inputs["norm_ffn1"]).reshape(2, 8, 128),
        "norm_mix": f32(inputs["norm_mix"]).reshape(2, 8, 128),
        "norm_ffn2": f32(inputs["norm_ffn2"]).reshape(2, 8, 128),
        "ffn1_wi": f32(inputs["ffn1_wi"]), "ffn1_wo": f32(inputs["ffn1_wo"]),
        "ffn2_wi": f32(inputs["ffn2_wi"]), "ffn2_wo": f32(inputs["ffn2_wo"]),
        "w_in": f32(inputs["w_in"]), "w_out": f32(inputs["w_out"]),
        "q_norm": f32(inputs["q_norm"]).reshape(2, 1, 64),
        "k_norm": f32(inputs["k_norm"]).reshape(2, 1, 64),
        "lambda_qk": f32(inputs["lambda_qk"]).reshape(2, 1, 256),
        "subln": f32(inputs["subln"]).reshape(2, 1, 128),
        "rel_bias": f32(inputs["rel_bias"]).reshape(32, 8),
    }
    for k in ("ident", "identb", "jmat", "onehot", "sel15", "sel31", "selL", "selR", "onesbd", "bdc", "bds"):
        shared[k] = c[k]
    in_maps = []
    for i in range(NCORES):
        m = dict(shared)
        if i < 4:
            m["x"] = xs[i]
            m["c2"] = np.ascontiguousarray(np.stack([cs_[i], cs_[i]]).reshape(16, 128))
            m["mask"] = c["mask_s"]
            m["dftm"] = c["dft_s"]
        else:
            p = 2 * (i - 4)
            m["x"] = np.ascontiguousarray(xp[p:p + 2].reshape(NT, D))
            m["c2"] = np.ascontiguousarray(cp_[p:p + 2].reshape(16, 128))
            m["mask"] = c["mask_p"]
            m["dftm"] = c["dft_p"]
        in_maps.append(m)
    nc = _get_nc(phases)
    res = run_bass_kernel_spmd(nc, in_maps, core_ids=list(range(NCORES)))
    ys = [np.asarray(r["y"], dtype=np.float32) for r in res.results]
    y_sample = np.stack(ys[0:4]).reshape(4, 4096, D)
    y_prompt = np.stack([ys[4 + i].reshape(2, 2048, D) for i in range(4)]).reshape(8, 2048, D)
    return (y_prompt, y_sample)


def kernel(**inputs):
    return _run(inputs, FULL_PHASES)
```
